# Optimizing a Trainium2 kernel written in Bass

```python
import math
import jax, jax.numpy as jnp
from jax import lax
import numpy as np

D_MODEL = 1024
BATCH = 16
SEQ = 2048
DEPTH = 1

N_DIFF_HEADS = 4
DIFF_HEAD_DIM = 64
DIFF_VDIM = 2 * DIFF_HEAD_DIM
DIFF_WIDTH = N_DIFF_HEADS * DIFF_VDIM
SGU_WIDTH = D_MODEL - DIFF_WIDTH
SGU_GROUPS = 4
SGU_GROUP_DIM = SGU_WIDTH // SGU_GROUPS
CHUNK = 128
Q_BLOCK = 128
ROT_DIM = DIFF_HEAD_DIM // 4
ROPE_THETA = 500000.0
D_FF = ((-(-8 * D_MODEL // 3)) + 255) // 256 * 256
ALPHA = (2 * DEPTH) ** 0.25
BETA = (8 * DEPTH) ** -0.25
LN_EPS = 1e-5
QKV_COLS = DIFF_WIDTH
PROJ_COLS = 3 * QKV_COLS + 2 * SGU_WIDTH

kernel_name = "hybrid_diffattn_sgu_deepnorm_adaln"


def _layernorm(x, g, b):
    xf = x.astype(jnp.float32)
    mu = jnp.mean(xf, axis=-1, keepdims=True)
    var = jnp.mean(jnp.square(xf - mu), axis=-1, keepdims=True)
    y = (xf - mu) * lax.rsqrt(var + LN_EPS)
    return (y * g.astype(jnp.float32) + b.astype(jnp.float32)).astype(x.dtype)


def _rmsnorm(x, g):
    xf = x.astype(jnp.float32)
    y = xf * lax.rsqrt(jnp.mean(jnp.square(xf), axis=-1, keepdims=True) + LN_EPS)
    return (y * g.astype(jnp.float32)).astype(x.dtype)


def _rope_tables(seq):
    half = ROT_DIM // 2
    inv_freq = ROPE_THETA ** (-jnp.arange(half, dtype=jnp.float32) * 2.0 / ROT_DIM)
    ang = jnp.arange(seq, dtype=jnp.float32)[:, None] * inv_freq[None, :]
    return jnp.cos(ang), jnp.sin(ang)


def _partial_rope(t, cos, sin):
    half = ROT_DIM // 2
    cs = cos[None, :, None, :].astype(t.dtype)
    sn = sin[None, :, None, :].astype(t.dtype)
    t1, t2, tp = t[..., :half], t[..., half:ROT_DIM], t[..., ROT_DIM:]
    return jnp.concatenate([t1 * cs - t2 * sn, t1 * sn + t2 * cs, tp], axis=-1)


def _diff_attention(q, k, v, lam, lam_init, subln_g, cos, sin):
    B, S, _ = q.shape
    H, d = N_DIFF_HEADS, DIFF_HEAD_DIM
    q = _partial_rope(q.reshape(B, S, 2 * H, d), cos, sin)
    k = _partial_rope(k.reshape(B, S, 2 * H, d), cos, sin)
    q = (q * (d ** -0.5)).reshape(B, S, H, 2, d).transpose(0, 2, 3, 1, 4)
    k = k.reshape(B, S, H, 2, d).transpose(0, 2, 3, 1, 4)
    v = v.reshape(B, S, H, DIFF_VDIM).transpose(0, 2, 1, 3)
    n_blocks = S // Q_BLOCK
    kpos = jnp.arange(S)

    def block(i):
        qi = lax.dynamic_slice_in_dim(q, i * Q_BLOCK, Q_BLOCK, axis=3)
        s = jnp.einsum('bhcqd,bhckd->bhcqk', qi, k).astype(jnp.float32)
        qpos = i * Q_BLOCK + jnp.arange(Q_BLOCK)
        mask = kpos[None, :] <= qpos[:, None]
        s = jnp.where(mask, s, jnp.finfo(jnp.float32).min)
        p = jax.nn.softmax(s, axis=-1)
        a = (p[:, :, 0] - lam * p[:, :, 1]).astype(v.dtype)
        return jnp.einsum('bhqk,bhke->bhqe', a, v)

    o = lax.map(block, jnp.arange(n_blocks))
    o = o.transpose(1, 0, 3, 2, 4).reshape(B, S, H, DIFF_VDIM)
    o = _rmsnorm(o, subln_g) * (1.0 - lam_init)
    return o.reshape(B, S, DIFF_WIDTH)


def _spatial_gating(z, ln_g, ln_b, w_s, b_s):
    B, S, _ = z.shape
    u, v = z[..., :SGU_WIDTH], z[..., SGU_WIDTH:]
    v = _layernorm(v, ln_g, ln_b)
    v = v.reshape(B, S // CHUNK, CHUNK, SGU_GROUPS, SGU_GROUP_DIM)
    causal = jnp.tril(jnp.ones((CHUNK, CHUNK), dtype=bool))
    w = jnp.where(causal[None], w_s, jnp.zeros_like(w_s))
    sv = jnp.einsum('gts,bnsgc->bntgc', w, v) + b_s.T[None, None, :, :, None]
    return u * sv.reshape(B, S, SGU_WIDTH)


def _swiglu(h, w_gate, w_up, w_down):
    return (jax.nn.silu(h @ w_gate) * (h @ w_up)) @ w_down


def setup_inputs(seed: int = 0) -> dict:
    key = jax.random.key(seed)
    ks = jax.random.split(key, 24)
    D = D_MODEL
    f32 = jnp.float32
    nrm = lambda k, shape, s: (jax.random.normal(k, shape, f32) * s)
    x = nrm(ks[0], (BATCH, SEQ, D), 1.0)
    c = nrm(ks[1], (BATCH, D), 1.0)
    ada_w = nrm(ks[2], (DEPTH, D, 6 * D), D ** -0.5)
    ada_b = nrm(ks[3], (DEPTH, 6 * D), 0.02)
    w_qk = nrm(ks[4], (DEPTH, D, 2 * QKV_COLS), D ** -0.5)
    w_vs = nrm(ks[5], (DEPTH, D, QKV_COLS + 2 * SGU_WIDTH), D ** -0.5 * BETA)
    w_in = jnp.concatenate([w_qk, w_vs], axis=-1)
    lambda_q1 = nrm(ks[6], (DEPTH, DIFF_HEAD_DIM), 0.1)
    lambda_k1 = nrm(ks[7], (DEPTH, DIFF_HEAD_DIM), 0.1)
    lambda_q2 = nrm(ks[8], (DEPTH, DIFF_HEAD_DIM), 0.1)
    lambda_k2 = nrm(ks[9], (DEPTH, DIFF_HEAD_DIM), 0.1)
    subln_g = 1.0 + nrm(ks[10], (DEPTH, DIFF_VDIM), 0.02)
    sgu_ln_g = 1.0 + nrm(ks[11], (DEPTH, SGU_WIDTH), 0.02)
    sgu_ln_b = nrm(ks[12], (DEPTH, SGU_WIDTH), 0.02)
    sgu_w = nrm(ks[13], (DEPTH, SGU_GROUPS, CHUNK, CHUNK), CHUNK ** -0.5)
    sgu_b = 1.0 + nrm(ks[14], (DEPTH, SGU_GROUPS, CHUNK), 0.1)
    w_o = nrm(ks[15], (DEPTH, D, D), D ** -0.5 * BETA)
    ln1_g = 1.0 + nrm(ks[16], (DEPTH, D), 0.02)
    ln1_b = nrm(ks[17], (DEPTH, D), 0.02)
    w_gate = nrm(ks[18], (DEPTH, D, D_FF), D ** -0.5 * BETA)
    w_up = nrm(ks[19], (DEPTH, D, D_FF), D ** -0.5 * BETA)
    w_down = nrm(ks[20], (DEPTH, D_FF, D), D_FF ** -0.5 * BETA)
    ln2_g = 1.0 + nrm(ks[21], (DEPTH, D), 0.02)
    ln2_b = nrm(ks[22], (DEPTH, D), 0.02)
    return {"x": x, "c": c, "ada_w": ada_w, "ada_b": ada_b, "w_in": w_in,
            "lambda_q1": lambda_q1, "lambda_k1": lambda_k1, "lambda_q2": lambda_q2,
            "lambda_k2": lambda_k2, "subln_g": subln_g, "sgu_ln_g": sgu_ln_g,
            "sgu_ln_b": sgu_ln_b, "sgu_w": sgu_w, "sgu_b": sgu_b, "w_o": w_o,
            "ln1_g": ln1_g, "ln1_b": ln1_b, "w_gate": w_gate, "w_up": w_up,
            "w_down": w_down, "ln2_g": ln2_g, "ln2_b": ln2_b}


def reference(x, c, ada_w, ada_b, w_in, lambda_q1, lambda_k1, lambda_q2, lambda_k2,
              subln_g, sgu_ln_g, sgu_ln_b, sgu_w, sgu_b, w_o, ln1_g, ln1_b,
              w_gate, w_up, w_down, ln2_g, ln2_b):
    B, S, D = x.shape
    cos, sin = _rope_tables(S)
    c_act = jax.nn.silu(c)
    o1, o2, o3 = QKV_COLS, 2 * QKV_COLS, 3 * QKV_COLS
    for l in range(DEPTH):
        lam_init = 0.8 - 0.6 * math.exp(-0.3 * l)
        mod = c_act @ ada_w[l] + ada_b[l]
        shift1, scale1, gate1, shift2, scale2, gate2 = [m[:, None, :] for m in jnp.split(mod, 6, axis=-1)]

        h = x * (1.0 + scale1) + shift1
        proj = h @ w_in[l]
        lam = (jnp.exp(jnp.sum(lambda_q1[l] * lambda_k1[l]).astype(jnp.float32))
               - jnp.exp(jnp.sum(lambda_q2[l] * lambda_k2[l]).astype(jnp.float32)) + lam_init)
        attn_out = _diff_attention(proj[..., :o1], proj[..., o1:o2], proj[..., o2:o3],
                                   lam, lam_init, subln_g[l], cos, sin)
        sgu_out = _spatial_gating(jax.nn.gelu(proj[..., o3:], approximate=False),
                                  sgu_ln_g[l], sgu_ln_b[l], sgu_w[l], sgu_b[l])
        mix = jnp.concatenate([attn_out, sgu_out], axis=-1) @ w_o[l]
        x = _layernorm(ALPHA * x + gate1 * mix, ln1_g[l], ln1_b[l])

        h2 = x * (1.0 + scale2) + shift2
        ffn = _swiglu(h2, w_gate[l], w_up[l], w_down[l])
        x = _layernorm(ALPHA * x + gate2 * ffn, ln2_g[l], ln2_b[l])
    return x
```

```python
import math
import os
KSTOP = int(os.environ.get('KSTOP', '0'))


class Stop(Exception):
    pass


def chk(n):
    if KSTOP == n:
        raise Stop()
from contextlib import ExitStack
import numpy as np
import concourse.bass as bass
import concourse.mybir as mybir
from concourse.bass_utils import run_bass_kernel_spmd

F32 = mybir.dt.float32
BF16 = mybir.dt.bfloat16
AF = mybir.ActivationFunctionType
ALU = mybir.AluOpType
AX = mybir.AxisListType

D = 1024
S = 2048
NBC = 2
NCORES = 8
DFF = 2816
NF = DFF // 128
PROJ = 2560
ALPHA = 2.0 ** 0.25
EPS = 1e-5
LAM_INIT = 0.2
ROPE_THETA = 500000.0


class Trk:
    __slots__ = ("w", "r", "x")

    def __init__(self, excl=False):
        self.w = None
        self.r = {}
        self.x = excl


class Eng:
    def __init__(self, name, h, sem):
        self.name, self.h, self.sem, self.cnt, self.waited = name, h, sem, 0, {}


class Prog:
    def __init__(self, nc, es):
        self.nc = nc
        mk = lambda n: es.enter_context(nc.semaphore(n))
        self.pe = Eng("pe", nc.tensor, mk("s_pe"))
        self.act = Eng("act", nc.scalar, mk("s_act"))
        self.dve = Eng("dve", nc.vector, mk("s_dve"))
        self.pool = Eng("pool", nc.gpsimd, mk("s_pool"))
        self.sp = Eng("sp", nc.sync, None)
        self.engs = [self.pe, self.act, self.dve, self.pool, self.sp]
        self.sp_sems = [[mk("s_sp%d" % i), 0] for i in range(16)]
        self.pq_sems = [[mk("s_pq%d" % i), 0] for i in range(8)]
        self.sp_i = 0
        self.pq_i = 0
        self.semcnt = {}

    def _emit_waits(self, eng, deps, fn):
        need = []
        for k, (s, v) in deps.items():
            if eng is self.pe and s is self.pe.sem:
                continue
            if eng.waited.get(k, 0) >= v:
                continue
            assert self.semcnt.get(k, (None, 0))[1] >= v, "wait on a signal never emitted (%s)" % eng.name
            need.append((s, v))
            eng.waited[k] = v
        for (s, v) in need[:-1]:
            eng.h.wait_ge(s, v)
        inst = fn()
        if need:
            inst._wait_ge(need[-1][0], need[-1][1])
        return inst

    @staticmethod
    def _deps(reads, writes, extra=(), own=None):
        deps = {}

        def add(ev):
            if ev is None:
                return
            k = id(ev[0])
            if k not in deps or deps[k][1] < ev[1]:
                deps[k] = ev
        for t in reads:
            add(t.w)
            if t.x:
                for ev in t.r.values():
                    if ev[0] is not own:
                        add(ev)
        for t in writes:
            add(t.w)
            for ev in t.r.values():
                add(ev)
        for ev in extra:
            add(ev)
        return deps

    @staticmethod
    def _record(ev, reads, writes):
        for t in writes:
            t.w = ev
            t.r = {}
        for t in reads:
            k = id(ev[0])
            if k not in t.r or t.r[k][1] < ev[1]:
                t.r[k] = ev

    def op(self, eng, fn, reads=(), writes=(), signal=True):
        deps = self._deps(reads, writes, own=eng.sem)
        inst = self._emit_waits(eng, deps, fn)
        if signal:
            eng.cnt += 1
            inst.then_inc(eng.sem, 1)
            self.semcnt[id(eng.sem)] = (eng.sem, eng.cnt)
            ev = (eng.sem, eng.cnt)
        else:
            ev = (eng.sem, eng.cnt + 1)
        self._record(ev, reads, writes)
        return inst

    def dma(self, q, out, in_, reads=(), writes=(), **kw):
        if q == "sp":
            eng, pool = self.sp, self.sp_sems
            i = self.sp_i
            self.sp_i = (i + 1) % len(pool)
        else:
            eng, pool = self.pool, self.pq_sems
            i = self.pq_i
            self.pq_i = (i + 1) % len(pool)
        sem, cnt = pool[i]
        extra = [(sem, cnt)] if cnt > 0 else []
        deps = self._deps(reads, writes, extra)
        inst = self._emit_waits(eng, deps, lambda: eng.h.dma_start(out=out, in_=in_, **kw))
        inst.then_inc(sem, 16)
        pool[i][1] = cnt + 16
        self.semcnt[id(sem)] = (sem, cnt + 16)
        ev = (sem, cnt + 16)
        self._record(ev, reads, writes)

    def all_events(self):
        evs = []
        for e in (self.pe, self.act, self.dve, self.pool):
            if e.cnt > 0:
                evs.append((e.sem, e.cnt))
        for s, c in self.sp_sems + self.pq_sems:
            if c > 0:
                evs.append((s, c))
        return evs

    def barrier(self):
        evs = self.all_events()
        for e in self.engs:
            for (s, v) in evs:
                if e.waited.get(id(s), 0) >= v:
                    continue
                e.h.wait_ge(s, v)
                e.waited[id(s)] = v


def build_nc():
    nc = bass.Bass("TRN2", target_bir_lowering=False)
    dt = lambda n, shp, kind="ExternalInput": nc.dram_tensor(n, shp, F32, kind=kind).ap()
    x_d = dt("x", [NBC * S, D])
    cT_d = dt("cT", [D, NBC])
    adaw_d = dt("ada_w", [D, 6 * D])
    adab_d = dt("ada_b", [1, 6 * D])
    win_d = dt("w_in", [D, PROJ])
    lam_d = dt("lam4", [1, 256])
    subg_d = dt("subln_g", [128, 1])
    sglg_d = dt("sgu_ln_g", [1, 512])
    sglb_d = dt("sgu_ln_b", [1, 512])
    sgwT_d = dt("sgu_wT", [4, 128, 128])
    sgb_d = dt("sgu_b", [1, 512])
    wo_d = dt("w_o", [D, D])
    ln1g_d = dt("ln1_g", [1, D])
    ln1b_d = dt("ln1_b", [1, D])
    ln1gT_d = dt("ln1_gT", [128, 8])
    ln1bT_d = dt("ln1_bT", [128, 8])
    wg_d = dt("w_gate", [D, DFF])
    wu_d = dt("w_up", [D, DFF])
    wd_d = dt("w_down", [DFF, D])
    ln2g_d = dt("ln2_g", [1, D])
    ln2b_d = dt("ln2_b", [1, D])
    ident_d = dt("ident", [128, 128])
    tri_d = dt("tri", [128, 128])
    cos_d = dt("cos_t", [128, 1024])
    sin_d = dt("sin_t", [128, 1024])
    sel_d = dt("sel", [2, 256])
    y_d = dt("y", [NBC * S, D], kind="ExternalOutput")
    g2s_d = dt("g2s", [2, D], kind="ExternalOutput")

    with ExitStack() as es:
        P = Prog(nc, es)
        pe, act, dve, pool = P.pe, P.act, P.dve, P.pool

        def sb(name, shape, dtype, stack=es):
            return stack.enter_context(nc.sbuf_tensor(name, shape, dtype))

        psum_all = es.enter_context(nc.psum_tensor("psum_all", [128, 4096], F32))
        banks = [psum_all[:, i * 512:(i + 1) * 512] for i in range(8)]
        bank_t = [Trk(True) for _ in range(8)]

        ident_f = sb("ident_f", [128, 128], F32)
        epsc = sb("epsc", [128, 1], F32)
        mhalf = sb("mhalf", [128, 1], F32)
        modT = sb("modT", [128, 4, 8, 2], F32)
        t_const = Trk()
        t_modT = Trk()
        t_g2s = Trk()

        P.dma("sp", ident_f[:], ident_d[:, :], writes=[t_const])
        P.op(dve, lambda: nc.vector.memset(epsc[:], EPS), writes=[t_const])
        P.op(dve, lambda: nc.vector.memset(mhalf[:], -0.5), writes=[t_const])

        def evac_copy(i, out, in_, reads, writes):
            if i % 2 == 0:
                P.op(act, lambda: nc.scalar.activation(out=out, in_=in_, func=AF.Copy), reads=reads, writes=writes)
            else:
                P.op(dve, lambda: nc.vector.tensor_copy(out=out, in_=in_), reads=reads, writes=writes)

        def layernorm_inplace(buf, t_buf, width, gbc, bbc, t_gb, small, t_small):
            nchunk = width // 512
            for c in range(nchunk):
                P.op(dve, lambda c=c: nc.vector.bn_stats(out=small[:, c * 6:(c + 1) * 6], in_=buf[:, c * 512:(c + 1) * 512]),
                     reads=[t_buf], writes=[t_small])
            P.op(dve, lambda: nc.vector.bn_aggr(out=small[:, 12:14], in_=small[:, 0:6 * nchunk]), reads=[t_small], writes=[t_small])
            P.op(dve, lambda: nc.vector.tensor_scalar(out=small[:, 14:15], in0=small[:, 13:14], scalar1=EPS, scalar2=None, op0=ALU.add),
                 reads=[t_small], writes=[t_small])
            P.op(pool, lambda: nc.gpsimd.tensor_tensor(out=small[:, 15:16], in0=small[:, 14:15], in1=mhalf[:], op=ALU.pow),
                 reads=[t_small, t_const], writes=[t_small])
            P.op(dve, lambda: nc.vector.tensor_scalar(out=buf[:, 0:width], in0=buf[:, 0:width], scalar1=small[:, 12:13], scalar2=small[:, 15:16],
                                                      op0=ALU.subtract, op1=ALU.mult), reads=[t_buf, t_small], writes=[t_buf])
            if gbc is not None:
                P.op(dve, lambda: nc.vector.tensor_tensor(out=buf[:, 0:width], in0=buf[:, 0:width], in1=gbc, op=ALU.mult),
                     reads=[t_buf, t_gb], writes=[t_buf])

        stopped = [False]
        with ExitStack() as esA:
            try:
                sA = lambda n, shp, d: sb(n, shp, d, esA)
                ident_b = sA("ident_b", [128, 128], BF16)
                tri_f = sA("tri_f", [128, 128], F32)
                tri_b = sA("tri_b", [128, 128], BF16)
                ones_b = sA("ones_b", [128, 128], BF16)
                ones_f = sA("ones_f", [128, 128], F32)
                P.dma("sp", tri_f[:], tri_d[:, :], writes=[t_const])
                P.op(dve, lambda: nc.vector.tensor_copy(out=ident_b[:], in_=ident_f[:]), reads=[t_const], writes=[t_const])
                P.op(dve, lambda: nc.vector.tensor_copy(out=tri_b[:], in_=tri_f[:]), reads=[t_const], writes=[t_const])
                P.op(dve, lambda: nc.vector.memset(ones_b[:], 1.0), writes=[t_const])
                P.op(dve, lambda: nc.vector.memset(ones_f[:], 1.0), writes=[t_const])
                w_in = sA("w_in_sb", [128, 8, PROJ], BF16)
                w_o = sA("w_o_sb", [128, 8, D], BF16)
                t_win = [Trk() for _ in range(8)]
                t_wo = [Trk() for _ in range(8)]
                g1bc = sA("g1bc", [128, 2, 1024], F32)
                t_g1 = Trk()
                sglg = sA("sglg", [128, 512], F32)
                sglb = sA("sglb", [128, 512], F32)
                cos_t = sA("cos_sb", [128, 1024], F32)
                sin_t = sA("sin_sb", [128, 1024], F32)
                WT = sA("WT", [128, 4, 128], BF16)
                sgub = sA("sgub", [1, 512], F32)
                Pt2 = [sA("Pt2_%d" % i, [128, 2, 512], BF16) for i in range(2)]
                t_Pt2 = [Trk(), Trk()]
                Pt = [[Pt2[i][:, c, :] for i in range(2)] for c in range(2)]
                t_Pt = [[t_Pt2[i] for i in range(2)] for c in range(2)]
                tri_b2 = sA("tri_b2", [128, 2, 128], BF16)
                for c_ in range(2):
                    P.op(dve, lambda c_=c_: nc.vector.tensor_copy(out=tri_b2[:, c_, :], in_=tri_f[:]), reads=[t_const], writes=[t_const])
                lam4 = Pt2[0][:, 0, :].bitcast(F32)
                lamt = Pt2[1][:, 0, :].bitcast(F32)[:, 0:128]
                sel = g1bc[0:2, 1, 768:1024]
                lams = sA("lams", [128, 4], F32)
                nlam = sA("nlam", [128, 1], F32)
                gs = sA("gs", [128, 1], F32)
                t_pA = Trk()
                cTs = sA("cTs", [128, 8, 2], F32)
                cact = sA("cact", [128, 8, 2], BF16)
                t_c = Trk()
                t_stage = [Trk() for _ in range(2)]
                fin = [[sA("fin%d_%d" % (j, i), [128, 512], F32) for i in range(4)] for j in range(2)]
                t_fin = [[Trk() for _ in range(4)] for _ in range(2)]
                adab = [fin[1][i][0:1, :] for i in range(2)]
                t_adab = [t_fin[1][i] for i in range(2)]
                modblk = [fin[1][2 + i][0:2, :] for i in range(2)]
                t_modblk = [t_fin[1][2 + i] for i in range(2)]

                NXB = 5
                xbuf = [sA("xbuf%d" % i, [128, D], F32) for i in range(NXB)]
                t_xbuf = [Trk() for _ in range(NXB)]
                xb_i = [0]
                ytmp2 = [sA("ytmp%d" % i, [128, D], F32) for i in range(2)]
                t_ytmp2 = [Trk(), Trk()]
                small = sA("small", [128, 16], F32)
                t_small = Trk()
                small2 = sA("small2", [128, 16], F32)
                t_small2 = Trk()
                hT = sA("hT", [128, 8, 512], BF16)
                t_hT = [Trk() for _ in range(4)]
                uT = sA("uT", [128, 4, 512], BF16)
                t_uT = Trk()
                catT = sA("catT", [128, 8, 512], BF16)
                t_cat_att = [Trk() for _ in range(4)]
                t_cat_sgu = [Trk() for _ in range(4)]
                qT = sA("qT", [128, 4, 512], BF16)
                t_qT = [Trk() for _ in range(4)]
                kT = sA("kT", [128, 4, S], BF16)
                t_kT = [Trk() for _ in range(16)]
                stage = [kT[:, 2 * i:2 * i + 2, :].rearrange("p h (a n) -> p (h a) n", n=512) for i in range(2)]
                V = sA("V", [128, 16, 512], BF16)
                t_V = [Trk() for _ in range(16)]
                qk_tm2 = [sA("qk_tm%d" % i, [128, 1024], BF16) for i in range(2)]
                t_qktm2 = [[Trk(), Trk()] for _ in range(2)]
                rtmp = sA("rtmp", [128, 8, 64], F32)
                t_rtmp = [Trk(), Trk()]
                gsv = sA("gsv", [128, 512], F32)
                t_gsv = Trk()
                vln2 = [sA("vln%d" % i, [128, 512], BF16) for i in range(2)]
                t_vln2 = [Trk(), Trk()]

                chk(1)
                P.dma("sp", cTs[:], cT_d.rearrange("(kc p) b -> p kc b", p=128), writes=[t_c])
                P.dma("sp", sel, sel_d[:, :], writes=[t_g1])
                P.op(act, lambda: nc.scalar.activation(out=cact[:], in_=cTs[:], func=AF.Silu), reads=[t_c], writes=[t_c])

                chk(2)
                win_loaded = [False]

                def load_w_in():
                    for kc in range(8):
                        P.dma("pool", w_in[:, kc, :], win_d[kc * 128:(kc + 1) * 128, :], writes=[t_win[kc]], max_dma_last_dim=4096)

                def mod_block(j, sidx, stage_kc, t_stg, load_stage, bk, tbk, bbk):
                    load_stage(j)
                    P.dma("sp", adab[sidx], adab_d[0:1, j * 512:(j + 1) * 512], writes=[t_adab[sidx]])
                    for kc in range(8):
                        P.op(pe, lambda kc=kc: nc.tensor.matmul(banks[bk][0:2, :], cact[:, kc, :], stage_kc(kc),
                                                                start=(kc == 0), stop=False),
                             reads=[t_c] + t_stg, writes=[bank_t[bk]], signal=False)
                    P.op(pe, lambda: nc.tensor.matmul(banks[bk][0:2, :], ones_f[0:1, 0:2], adab[sidx], start=False, stop=True),
                         reads=[t_const, t_adab[sidx]], writes=[bank_t[bk]])
                    P.op(dve, lambda: nc.vector.tensor_copy(out=modblk[sidx], in_=banks[bk][0:2, :]),
                         reads=[bank_t[bk]], writes=[t_modblk[sidx]])
                    gi, hh = j // 2, j % 2
                    if gi in (0, 1, 3, 4):
                        grp = {0: 0, 1: 1, 3: 2, 4: 3}[gi]
                        for cc in range(4):
                            P.op(pe, lambda cc=cc: nc.tensor.matmul(banks[tbk][:, cc * 2:cc * 2 + 2], modblk[sidx][:, cc * 128:(cc + 1) * 128],
                                                                    ident_f[0:2, 0:2], start=True, stop=True),
                                 reads=[t_modblk[sidx], t_const], writes=[bank_t[tbk]], signal=(cc == 3))
                        src = banks[tbk][:, 0:8].rearrange("p (c b) -> p c b", b=2)
                        dst = modT[:, grp, hh * 4:(hh + 1) * 4, :]
                        if gi in (1, 4):
                            P.op(dve, lambda: nc.vector.tensor_scalar(out=dst, in0=src, scalar1=1.0, scalar2=None, op0=ALU.add),
                                 reads=[bank_t[tbk]], writes=[t_modT])
                        else:
                            P.op(dve, lambda: nc.vector.tensor_copy(out=dst, in_=src), reads=[bank_t[tbk]], writes=[t_modT])
                    elif gi == 2:
                        for b in range(2):
                            xbk = (tbk, bbk)[b]
                            P.op(pe, lambda b=b: nc.tensor.matmul(banks[xbk][:, :], sel[0:2, b * 128:(b + 1) * 128], modblk[sidx],
                                                                  start=True, stop=True),
                                 reads=[t_pA, t_modblk[sidx], t_g1], writes=[bank_t[xbk]])
                            P.op(act, lambda b=b: nc.scalar.activation(out=g1bc[:, b, hh * 512:(hh + 1) * 512], in_=banks[xbk][:, :], func=AF.Copy),
                                 reads=[bank_t[xbk]], writes=[t_g1])
                    else:
                        P.dma("sp", g2s_d[0:2, hh * 512:(hh + 1) * 512], modblk[sidx], reads=[t_modblk[sidx]], writes=[t_g2s])

                for jj, j in enumerate([2, 3, 0, 1]):
                    sidx = jj % 2

                    def ld(j, sidx=sidx):
                        P.dma("pool", stage[sidx], adaw_d[:, j * 512:(j + 1) * 512].rearrange("(kc p) n -> p kc n", p=128),
                              writes=[t_stage[sidx]])
                    mod_block(j, sidx, lambda kc, sidx=sidx: stage[sidx][:, kc, :], [t_stage[sidx]], ld, jj % 2, 2 + (jj % 2), 4 + (jj % 2))
                    if jj == 1:
                        load_w_in()

                chk(4)
                for t in t_kT:
                    for ts in t_stage:
                        for k, ev in ts.r.items():
                            if k not in t.r or t.r[k][1] < ev[1]:
                                t.r[k] = ev
                        if ts.w is not None and (t.w is None or True):
                            t.r[id(ts.w[0])] = ts.w

                chk(5)
                P.dma("sp", cos_t[:], cos_d[:, :], writes=[t_pA])
                P.dma("sp", sin_t[:], sin_d[:, :], writes=[t_pA])
                P.dma("sp", sglg[:], sglg_d[0:1, :].partition_broadcast(128), writes=[t_pA])
                P.dma("sp", sglb[:], sglb_d[0:1, :].partition_broadcast(128), writes=[t_pA])
                P.dma("sp", sgub[:], sgb_d[0:1, :], writes=[t_pA])
                WTf = gsv[:].rearrange("p (g t) -> p g t", t=128)
                P.dma("sp", WTf, sgwT_d.rearrange("g s t -> s g t"), writes=[t_gsv])
                P.dma("sp", lam4, lam_d[0:1, :].partition_broadcast(128), writes=[t_pA, t_Pt[0][0]])
                P.dma("sp", gs[:], subg_d[:, :], writes=[t_pA])
                for g in range(4):
                    P.op(dve, lambda g=g: nc.vector.tensor_tensor(out=WT[:, g, :], in0=WTf[:, g, :], in1=tri_f[:], op=ALU.mult),
                         reads=[t_pA, t_const, t_gsv], writes=[t_pA])
                P.op(dve, lambda: nc.vector.tensor_scalar(out=gs[:], in0=gs[:], scalar1=1.0 - LAM_INIT, scalar2=None, op0=ALU.mult),
                     reads=[t_pA], writes=[t_pA])
                P.op(dve, lambda: nc.vector.tensor_tensor(out=lamt[:, 0:64], in0=lam4[:, 0:64], in1=lam4[:, 64:128], op=ALU.mult),
                     reads=[t_pA, t_Pt[0][0]], writes=[t_pA, t_Pt[0][1]])
                P.op(dve, lambda: nc.vector.tensor_tensor(out=lamt[:, 64:128], in0=lam4[:, 128:192], in1=lam4[:, 192:256], op=ALU.mult),
                     reads=[t_pA, t_Pt[0][0]], writes=[t_pA, t_Pt[0][1]])
                P.op(dve, lambda: nc.vector.tensor_reduce(out=lams[:, 0:2], in_=lamt.rearrange("p (a d) -> p a d", d=64), axis=AX.X, op=ALU.add), reads=[t_pA, t_Pt[0][1]], writes=[t_pA])
                P.op(act, lambda: nc.scalar.activation(out=lams[:, 2:4], in_=lams[:, 0:2], func=AF.Exp), reads=[t_pA], writes=[t_pA])
                P.op(dve, lambda: nc.vector.tensor_tensor(out=nlam[:], in0=lams[:, 3:4], in1=lams[:, 2:3], op=ALU.subtract),
                     reads=[t_pA], writes=[t_pA])
                P.op(dve, lambda: nc.vector.tensor_scalar(out=nlam[:], in0=nlam[:], scalar1=-LAM_INIT, scalar2=None, op0=ALU.add),
                     reads=[t_pA], writes=[t_pA])

                chk(6)
                mm_i = [0]
                tr_i = [0]
                a1_i = [0]

                def next_mm():
                    mm_i[0] = (mm_i[0] + 1) % 4
                    return mm_i[0]

                def next_tr():
                    tr_i[0] ^= 1
                    return 6 + tr_i[0]

                def rope_evac(bk, which, pos_tile, par):
                    qk_tm, t_qktm = qk_tm2[par], t_qktm2[par]
                    src = banks[bk][:, :].rearrange("p (c d) -> p c d", d=64)
                    dst = qk_tm[:, which * 512:(which + 1) * 512].rearrange("p (c d) -> p c d", d=64)
                    cs = cos_t[:, pos_tile * 64:(pos_tile + 1) * 64].rearrange("p (c d) -> p c d", d=8)
                    sn = sin_t[:, pos_tile * 64:(pos_tile + 1) * 64].rearrange("p (c d) -> p c d", d=8)
                    t1, t2 = src[:, :, 0:8], src[:, :, 8:16]
                    r = [rtmp[:, which * 4 + i, :].rearrange("p (c d) -> p c d", d=8) for i in range(4)]
                    tq = t_qktm[which]
                    trt = t_rtmp[which]
                    P.op(act, lambda: nc.scalar.activation(out=dst[:, :, 16:64], in_=src[:, :, 16:64], func=AF.Copy),
                         reads=[bank_t[bk]], writes=[tq])
                    P.op(dve, lambda: nc.vector.tensor_tensor(out=r[0], in0=t1, in1=cs, op=ALU.mult), reads=[bank_t[bk], t_pA], writes=[trt])
                    P.op(dve, lambda: nc.vector.tensor_tensor(out=r[1], in0=t2, in1=sn, op=ALU.mult), reads=[bank_t[bk], t_pA], writes=[trt])
                    P.op(dve, lambda: nc.vector.tensor_tensor(out=r[2], in0=t1, in1=sn, op=ALU.mult), reads=[bank_t[bk], t_pA], writes=[trt])
                    P.op(dve, lambda: nc.vector.tensor_tensor(out=r[3], in0=t2, in1=cs, op=ALU.mult), reads=[bank_t[bk], t_pA], writes=[trt])
                    P.op(dve, lambda: nc.vector.tensor_tensor(out=dst[:, :, 0:8], in0=r[0], in1=r[1], op=ALU.subtract), reads=[trt], writes=[tq])
                    P.op(dve, lambda: nc.vector.tensor_tensor(out=dst[:, :, 8:16], in0=r[2], in1=r[3], op=ALU.add), reads=[trt], writes=[tq])

                def A1_tile(gb, tt):
                    b, tb = divmod(gb, 4)
                    xi = xb_i[0]
                    xb_i[0] = (xi + 1) % NXB
                    r0 = gb * 512 + tt * 128
                    xt, t_xt = xbuf[xi], t_xbuf[xi]
                    P.dma("sp", xt[:], x_d[r0:r0 + 128, :], writes=[t_xt])
                    for half in range(2):
                        a1_i[0] = (a1_i[0] + 1) % 4
                        bk = 4 + a1_i[0]
                        for cc in range(4):
                            kc = half * 4 + cc
                            P.op(pe, lambda cc=cc, kc=kc: nc.tensor.transpose(banks[bk][:, cc * 128:(cc + 1) * 128],
                                                                              xt[:, kc * 128:(kc + 1) * 128], ident_f[:]),
                                 reads=[t_xt, t_const], writes=[bank_t[bk]], signal=(cc == 3))
                        for cc in range(4):
                            kc = half * 4 + cc
                            o = hT[:, kc, tt * 128:(tt + 1) * 128]
                            i_ = banks[bk][:, cc * 128:(cc + 1) * 128]
                            sc = modT[:, 1, kc, b:b + 1]
                            sh = modT[:, 0, kc, b:b + 1]
                            if bk % 2 == 0:
                                P.op(act, lambda o=o, i_=i_, sc=sc, sh=sh: nc.scalar.activation(out=o, in_=i_, func=AF.Identity, scale=sc, bias=sh),
                                     reads=[bank_t[bk], t_modT], writes=[t_hT[tt]])
                            else:
                                P.op(dve, lambda o=o, i_=i_, sc=sc, sh=sh: nc.vector.tensor_scalar(out=o, in0=i_, scalar1=sc, scalar2=sh,
                                                                                                    op0=ALU.mult, op1=ALU.add),
                                     reads=[bank_t[bk], t_modT], writes=[t_hT[tt]])

                def A2a(gb):
                    for j in range(4):
                        bk = next_mm()
                        for kc in range(8):
                            P.op(pe, lambda kc=kc: nc.tensor.matmul(banks[bk][:, :], w_in[:, kc, 1536 + j * 128:1536 + (j + 1) * 128],
                                                                    hT[:, kc, :], start=(kc == 0), stop=(kc == 7)),
                                 reads=[t_win[kc]] + t_hT, writes=[bank_t[bk]], signal=(kc == 7))
                        P.op(act, lambda: nc.scalar.activation(out=uT[:, j, :], in_=banks[bk][:, :], func=AF.Gelu),
                             reads=[bank_t[bk]], writes=[t_uT])

                def A2b_mm(gb, tt):
                    b, tb = divmod(gb, 4)
                    ptile = tb * 4 + tt
                    vln, t_vln = vln2[tt % 2], t_vln2[tt % 2]
                    for grp, col0 in (("q", 0), ("k", 512), ("v", 1024), ("sv", 2048)):
                        bk = next_mm()
                        for kc in range(8):
                            P.op(pe, lambda kc=kc: nc.tensor.matmul(banks[bk][:, :], hT[:, kc, tt * 128:(tt + 1) * 128],
                                                                    w_in[:, kc, col0:col0 + 512], start=(kc == 0), stop=(kc == 7)),
                                 reads=[t_win[kc], t_hT[tt]], writes=[bank_t[bk]], signal=(kc == 7))
                        if grp == "q":
                            rope_evac(bk, 0, ptile, tt % 2)
                        elif grp == "k":
                            rope_evac(bk, 1, ptile, tt % 2)
                        elif grp == "v":
                            P.op(act, lambda: nc.scalar.activation(out=V[:, ptile, :], in_=banks[bk][:, :], func=AF.Copy), reads=[bank_t[bk]], writes=[t_V[ptile]])
                        else:
                            P.op(act, lambda: nc.scalar.activation(out=gsv[:], in_=banks[bk][:, :], func=AF.Gelu),
                                 reads=[bank_t[bk]], writes=[t_gsv])
                            P.op(dve, lambda: nc.vector.bn_stats(out=small[:, 0:6], in_=gsv[:]), reads=[t_gsv], writes=[t_small])
                            P.op(dve, lambda: nc.vector.bn_aggr(out=small[:, 12:14], in_=small[:, 0:6]), reads=[t_small], writes=[t_small])
                            P.op(dve, lambda: nc.vector.tensor_scalar(out=small[:, 14:15], in0=small[:, 13:14], scalar1=EPS, scalar2=None,
                                                                      op0=ALU.add), reads=[t_small], writes=[t_small])
                            P.op(pool, lambda: nc.gpsimd.tensor_tensor(out=small[:, 15:16], in0=small[:, 14:15], in1=mhalf[:], op=ALU.pow),
                                 reads=[t_small, t_const], writes=[t_small])
                            P.op(dve, lambda: nc.vector.tensor_scalar(out=gsv[:], in0=gsv[:], scalar1=small[:, 12:13], scalar2=small[:, 15:16],
                                                                      op0=ALU.subtract, op1=ALU.mult), reads=[t_gsv, t_small], writes=[t_gsv])
                            P.op(dve, lambda: nc.vector.tensor_tensor(out=gsv[:], in0=gsv[:], in1=sglg[:], op=ALU.mult),
                                 reads=[t_gsv, t_pA], writes=[t_gsv])
                            P.op(dve, lambda: nc.vector.tensor_tensor(out=vln[:], in0=gsv[:], in1=sglb[:], op=ALU.add),
                                 reads=[t_gsv, t_pA], writes=[t_vln])

                def A2b_dep(gb, tt):
                    b, tb = divmod(gb, 4)
                    ptile = tb * 4 + tt
                    vln, t_vln = vln2[tt % 2], t_vln2[tt % 2]
                    qk_tm, t_qktm = qk_tm2[tt % 2], t_qktm2[tt % 2]
                    for which in range(2):
                        tbk = next_tr()
                        bfv = banks[tbk][:, :].bitcast(BF16)
                        for h in range(4):
                            P.op(pe, lambda h=h: nc.tensor.transpose(bfv[:, h * 128:(h + 1) * 128],
                                                                     qk_tm[:, which * 512 + h * 128:which * 512 + (h + 1) * 128], ident_b[:]),
                                 reads=[t_qktm[which], t_const], writes=[bank_t[tbk]], signal=(h == 3))
                        srcv = bfv[:, 0:512].rearrange("p (h t) -> p h t", t=128)
                        if which == 0:
                            P.op(act, lambda: nc.scalar.activation(out=qT[:, :, tt * 128:(tt + 1) * 128], in_=srcv, func=AF.Copy),
                                 reads=[bank_t[tbk]], writes=[t_qT[tt]])
                        else:
                            P.op(act, lambda: nc.scalar.activation(out=kT[:, :, ptile * 128:(ptile + 1) * 128], in_=srcv, func=AF.Copy),
                                 reads=[bank_t[tbk]], writes=[t_kT[ptile]])
                    sbk = next_tr()
                    for g in range(4):
                        P.op(pe, lambda g=g: nc.tensor.matmul(banks[sbk][:, g * 128:(g + 1) * 128], vln[:, g * 128:(g + 1) * 128],
                                                              WT[:, g, :], start=True, stop=False),
                             reads=[t_vln, t_pA], writes=[bank_t[sbk]], signal=False)
                        P.op(pe, lambda g=g: nc.tensor.matmul(banks[sbk][:, g * 128:(g + 1) * 128], ones_f[0:1, :],
                                                              sgub[0:1, g * 128:(g + 1) * 128], start=False, stop=True),
                             reads=[t_const, t_pA], writes=[bank_t[sbk]], signal=(g == 3))
                    P.op(dve, lambda: nc.vector.tensor_tensor(out=catT[:, 4:8, tt * 128:(tt + 1) * 128],
                                                              in0=banks[sbk][:, :].rearrange("p (g t) -> p g t", t=128),
                                                              in1=uT[:, :, tt * 128:(tt + 1) * 128], op=ALU.mult),
                         reads=[bank_t[sbk], t_uT], writes=[t_cat_sgu[tt]])

                def fin_part1(h):
                    fs, tf = fin[h % 2], t_fin[h % 2]
                    P.op(dve, lambda: nc.vector.tensor_copy(out=fs[0][:], in_=banks[4][:, :]), reads=[bank_t[4]], writes=[tf[0]])
                    P.op(act, lambda: nc.scalar.activation(out=fs[2][:], in_=banks[6][:, :], func=AF.Copy), reads=[bank_t[6]], writes=[tf[2]])
                    P.op(dve, lambda: nc.vector.tensor_copy(out=fs[1][:], in_=banks[5][:, :]), reads=[bank_t[5]], writes=[tf[1]])
                    P.op(act, lambda: nc.scalar.activation(out=fs[3][:], in_=banks[7][:, :], func=AF.Copy), reads=[bank_t[7]], writes=[tf[3]])

                def fin_part2_ops(h):
                    fs, tf = fin[h % 2], t_fin[h % 2]
                    ops = []
                    for c in range(2):
                        ops.append(lambda bkf, c=c: P.op(dve, lambda: nc.vector.reciprocal(out=fs[2 + c][:], in_=fs[2 + c][:]),
                                                        reads=[tf[2 + c]], writes=[tf[2 + c]]))
                        ops.append(lambda bkf, c=c: P.op(dve, lambda: nc.vector.tensor_tensor(out=fs[c][:], in0=fs[c][:], in1=fs[2 + c][:], op=ALU.mult),
                                                        reads=[tf[c], tf[2 + c]], writes=[tf[c]]))
                    ops.append(lambda bkf: P.op(dve, lambda: nc.vector.scalar_tensor_tensor(out=fs[0][:], in0=fs[1][:], scalar=nlam[:, 0:1], in1=fs[0][:],
                                                                                            op0=ALU.mult, op1=ALU.add),
                                                reads=[tf[1], t_pA], writes=[tf[0]]))
                    ops.append(lambda bkf: P.op(dve, lambda: nc.vector.tensor_tensor(out=fs[2][:], in0=fs[0][:], in1=fs[0][:], op=ALU.mult),
                                                reads=[tf[0]], writes=[tf[2]]))

                    def rms(bkf):
                        P.op(pe, lambda: nc.tensor.matmul(banks[bkf][:, :], ones_f[:], fs[2][:], start=True, stop=True),
                             reads=[t_const, tf[2]], writes=[bank_t[bkf]])
                        P.op(act, lambda: nc.scalar.activation(out=fs[3][:], in_=banks[bkf][:, :], func=AF.Ln,
                                                               scale=1.0 / 128.0, bias=epsc[:, 0:1]),
                             reads=[bank_t[bkf], t_const], writes=[tf[3]])
                        P.op(act, lambda: nc.scalar.activation(out=fs[3][:], in_=fs[3][:], func=AF.Exp, scale=-0.5),
                             reads=[tf[3]], writes=[tf[3]])
                    ops.append(rms)
                    ops.append(lambda bkf: P.op(dve, lambda: nc.vector.scalar_tensor_tensor(out=catT[:, h, :], in0=fs[0][:], scalar=gs[:, 0:1], in1=fs[3][:],
                                                                                            op0=ALU.mult, op1=ALU.mult),
                                                reads=[tf[0], tf[3], t_pA], writes=[t_cat_att[h]]))
                    return ops

                def A3(gb):
                    b, tb = divmod(gb, 4)
                    nkt = 4 * tb + 4
                    steps = [(h, kt) for h in range(4) for kt in range(nkt)]

                    def q0_of(kt):
                        jd = kt - 4 * tb
                        return 0 if jd <= 0 else jd * 128

                    def S(i):
                        h, kt = steps[i]
                        q0 = q0_of(kt)
                        sb0 = (i % 2) * 2
                        for c in range(2):
                            lo, hi = c * 64, (c + 1) * 64
                            P.op(pe, lambda c=c, lo=lo, hi=hi: nc.tensor.matmul(banks[sb0 + c][:, q0:512], kT[lo:hi, h, kt * 128:(kt + 1) * 128],
                                                                                qT[lo:hi, h, q0:512], start=True, stop=True),
                                 reads=[t_kT[kt]] + t_qT, writes=[bank_t[sb0 + c]])

                    pending = []
                    S(0)
                    for i, (h, kt) in enumerate(steps):
                        if i + 1 < len(steps):
                            S(i + 1)
                        jd = kt - 4 * tb
                        q0 = q0_of(kt)
                        pi = i % 2
                        sb0 = pi * 2
                        s2 = psum_all[:, sb0 * 512:(sb0 + 2) * 512].rearrange("p (c n) -> p c n", n=512)
                        P.op(act, lambda: nc.scalar.activation(out=Pt2[pi][:, :, q0:512], in_=s2[:, :, q0:512], func=AF.Exp, scale=0.125),
                             reads=[bank_t[sb0], bank_t[sb0 + 1]], writes=[t_Pt2[pi]])
                        if jd >= 0:
                            P.op(dve, lambda: nc.vector.tensor_tensor(out=Pt2[pi][:, :, q0:q0 + 128], in0=Pt2[pi][:, :, q0:q0 + 128],
                                                                      in1=tri_b2[:], op=ALU.mult),
                                 reads=[t_Pt2[pi], t_const], writes=[t_Pt2[pi]])
                        for c in range(2):
                            P.op(pe, lambda c=c: nc.tensor.matmul(banks[4 + c][:, q0:512], V[:, kt, h * 128:(h + 1) * 128], Pt[c][pi][:, q0:512],
                                                                  start=(kt == 0), stop=(kt == nkt - 1)),
                                 reads=[t_V[kt], t_Pt[c][pi]], writes=[bank_t[4 + c]], signal=False)
                            P.op(pe, lambda c=c: nc.tensor.matmul(banks[6 + c][:, q0:512], ones_b[:], Pt[c][pi][:, q0:512],
                                                                  start=(kt == 0), stop=(kt == nkt - 1)),
                                 reads=[t_const, t_Pt[c][pi]], writes=[bank_t[6 + c]], signal=True)
                        if kt >= 1:
                            for _ in range(3):
                                if pending:
                                    pending.pop(0)(sb0)
                        if kt == nkt - 1:
                            while pending:
                                pending.pop(0)(sb0)
                            fin_part1(h)
                            pending = fin_part2_ops(h)
                    while pending:
                        pending.pop(0)(2)

                def A4_tile(gb, tt):
                    b, tb = divmod(gb, 4)
                    ytmp, t_ytmp = ytmp2[tt % 2], t_ytmp2[tt % 2]
                    xi = xb_i[0]
                    xb_i[0] = (xi + 1) % NXB
                    r0 = gb * 512 + tt * 128
                    xr, t_xr = xbuf[xi], t_xbuf[xi]
                    P.dma("sp", xr[:], x_d[r0:r0 + 128, :], writes=[t_xr])
                    for half in range(2):
                        bk = next_mm()
                        for kc in range(8):
                            rd = [t_wo[kc], t_cat_att[kc]] if kc < 4 else [t_wo[kc], t_cat_sgu[tt]]
                            P.op(pe, lambda kc=kc: nc.tensor.matmul(banks[bk][:, :], catT[:, kc, tt * 128:(tt + 1) * 128],
                                                                    w_o[:, kc, half * 512:(half + 1) * 512], start=(kc == 0), stop=(kc == 7)),
                                 reads=rd, writes=[bank_t[bk]], signal=(kc == 7))
                        P.op(dve, lambda half=half: nc.vector.tensor_tensor(out=ytmp[:, half * 512:(half + 1) * 512], in0=banks[bk][:, :],
                                                                            in1=g1bc[:, b, half * 512:(half + 1) * 512], op=ALU.mult),
                             reads=[bank_t[bk], t_g1], writes=[t_ytmp])
                    P.op(dve, lambda: nc.vector.scalar_tensor_tensor(out=xr[:], in0=xr[:], scalar=ALPHA, in1=ytmp[:], op0=ALU.mult, op1=ALU.add),
                         reads=[t_xr, t_ytmp], writes=[t_xr])
                    layernorm_inplace(xr, t_xr, D, None, None, t_pA, small2, t_small2)
                    P.dma("pool", y_d[r0:r0 + 128, :], xr[:], reads=[t_xr])

                deferred = [4, 5]

                def deferred_load(j):
                    for i in range(4):
                        P.dma("pool", fin[0][i][:].bitcast(BF16).rearrange("p (a n) -> p a n", n=512),
                              adaw_d[i * 256:(i + 1) * 256, j * 512:(j + 1) * 512].rearrange("(a p) n -> p a n", p=128),
                              writes=[t_fin[0][i]])

                def deferred_compute(j):
                    mod_block(j, 0, lambda kc: fin[0][kc // 2][:].bitcast(BF16)[:, (kc % 2) * 512:(kc % 2 + 1) * 512],
                              [t_fin[0][i] for i in range(4)], lambda j: None, 4, 5, 4)

                def emit_deferred():
                    if not deferred:
                        return
                    j = deferred.pop(0)
                    deferred_load(j)
                    deferred_compute(j)

                late = {1: 10, 2: 11, 3: 6, 4: 7, 5: 8, 6: 9}

                NGB = NBC * 4
                for gb in range(NGB):
                    prev = gb - 1
                    if gb in late:
                        deferred_load(late[gb])
                    for tt in range(4):
                        A1_tile(gb, tt)
                    chk(7)
                    if gb > 0:
                        A4_tile(prev, 0)
                    A2a(gb)
                    if gb == 0:
                        for kc in range(8):
                            P.dma("pool", w_o[:, kc, :], wo_d[kc * 128:(kc + 1) * 128, :], writes=[t_wo[kc]])
                    emit_deferred()
                    chk(8)
                    if gb > 0:
                        A4_tile(prev, 1)
                    A2b_mm(gb, 0)
                    emit_deferred()
                    if gb > 0:
                        A4_tile(prev, 2)
                    A2b_mm(gb, 1)
                    A2b_dep(gb, 0)
                    emit_deferred()
                    if gb > 0:
                        A4_tile(prev, 3)
                    A2b_mm(gb, 2)
                    A2b_dep(gb, 1)
                    emit_deferred()
                    A2b_mm(gb, 3)
                    A2b_dep(gb, 2)
                    A2b_dep(gb, 3)
                    if gb in late:
                        deferred_compute(late[gb])
                    chk(9)
                    A3(gb)
                    chk(10)
                for tt in range(4):
                    A4_tile(NGB - 1, tt)
            except Stop:
                stopped[0] = True
            P.barrier()

        with ExitStack() as esB:
          if not stopped[0]:
            sB = lambda n, shp, d: sb(n, shp, d, esB)
            wg = sB("wg_sb", [128, 8, DFF], BF16)
            wu = sB("wu_sb", [128, 8, DFF], BF16)
            wd = sB("wd_sb", [128, NF, D], BF16)
            t_wg = [Trk() for _ in range(11)]
            t_wu = [Trk() for _ in range(11)]
            t_wd = [Trk() for _ in range(11)]
            ln2g = sB("ln2g", [128, D], F32)
            ln2b = sB("ln2b", [128, D], F32)
            t_pB = Trk()
            h2T2 = [sB("h2T%d" % i, [128, 8, 512], BF16) for i in range(2)]
            t_h2T2 = [[Trk() for _ in range(4)] for _ in range(2)]
            gT = sB("gT", [128, NF, 512], BF16)
            t_gT = [Trk() for _ in range(NF)]
            NXBB = 2
            xbB = [sB("xbB_%d" % i, [128, D], F32) for i in range(NXBB)]
            t_xbB = [Trk() for _ in range(NXBB)]
            xbB_i = [0]
            ytmpB = sB("ytmpB", [128, D], F32)
            t_ytmpB = Trk()
            sgy = sB("sgy", [128, D], F32)
            sg = [sgy[:, 0:512], sgy[:, 512:1024]]
            t_sg = [Trk(), Trk()]
            g2cur = sB("g2cur", [128, D], F32)
            t_g2cur = Trk()
            agbc = sB("agbc", [128, D], F32)
            abbc = sB("abbc", [128, D], F32)
            gfm = sB("gfm", [128, 8], F32)
            bfm = sB("bfm", [128, 8], F32)
            modB = sB("modB", [128, 2, 8, 2], F32)
            smallB = sB("smallB", [128, 16], F32)
            t_smallB = Trk()

            P.dma("sp", ln2g[:], ln2g_d[0:1, :].partition_broadcast(128), writes=[t_pB])
            P.dma("sp", ln2b[:], ln2b_d[0:1, :].partition_broadcast(128), writes=[t_pB])
            P.dma("sp", agbc[:], ln1g_d[0:1, :].partition_broadcast(128), writes=[t_pB])
            P.dma("sp", abbc[:], ln1b_d[0:1, :].partition_broadcast(128), writes=[t_pB])
            P.dma("sp", gfm[:], ln1gT_d[:, :], writes=[t_pB])
            P.dma("sp", bfm[:], ln1bT_d[:, :], writes=[t_pB])
            P.op(dve, lambda: nc.vector.tensor_scalar(out=agbc[:], in0=agbc[:], scalar1=ALPHA, scalar2=None, op0=ALU.mult), reads=[t_pB], writes=[t_pB])
            P.op(dve, lambda: nc.vector.tensor_scalar(out=abbc[:], in0=abbc[:], scalar1=ALPHA, scalar2=None, op0=ALU.mult), reads=[t_pB], writes=[t_pB])
            for b_ in range(2):
                P.op(dve, lambda b_=b_: nc.vector.tensor_tensor(out=modB[:, 0, :, b_], in0=modT[:, 3, :, b_], in1=gfm[:], op=ALU.mult),
                     reads=[t_modT, t_pB], writes=[t_pB])
                P.op(dve, lambda b_=b_: nc.vector.tensor_tensor(out=modB[:, 1, :, b_], in0=modT[:, 3, :, b_], in1=bfm[:], op=ALU.mult),
                     reads=[t_modT, t_pB], writes=[t_pB])
                P.op(dve, lambda b_=b_: nc.vector.tensor_tensor(out=modB[:, 1, :, b_], in0=modB[:, 1, :, b_], in1=modT[:, 2, :, b_], op=ALU.add),
                     reads=[t_modT, t_pB], writes=[t_pB])
            wg_v = wg_d.rearrange("(kc p) n -> p kc n", p=128)
            wu_v = wu_d.rearrange("(kc p) n -> p kc n", p=128)
            wd_v = wd_d.rearrange("(f p) n -> p f n", p=128)
            for cgi in range(11):
                c0, c1 = cgi * 256, (cgi + 1) * 256
                P.dma("pool", wg[:, :, c0:c1], wg_v[:, :, c0:c1], writes=[t_wg[cgi]])
                P.dma("pool", wu[:, :, c0:c1], wu_v[:, :, c0:c1], writes=[t_wu[cgi]])
            for cgi in range(11):
                P.dma("pool", wd[:, 2 * cgi:2 * cgi + 2, :], wd_v[:, 2 * cgi:2 * cgi + 2, :], writes=[t_wd[cgi]])

            tbB = [0]

            def next_xb():
                i = xbB_i[0]
                xbB_i[0] = (i + 1) % NXBB
                return xbB[i], t_xbB[i]

            def TB_tile(blk, tt):
                b = blk // 4
                h2T, t_h2T = h2T2[blk % 2], t_h2T2[blk % 2]
                r0 = blk * 512 + tt * 128
                xt = sgy
                P.dma("sp", xt[:], y_d[r0:r0 + 128, :], writes=t_sg)
                for half in range(2):
                    tbB[0] = (tbB[0] + 1) % 4
                    bk = tbB[0]
                    for cc in range(4):
                        kc = half * 4 + cc
                        P.op(pe, lambda cc=cc, kc=kc: nc.tensor.transpose(banks[bk][:, cc * 128:(cc + 1) * 128],
                                                                          xt[:, kc * 128:(kc + 1) * 128], ident_f[:]),
                             reads=t_sg + [t_const], writes=[bank_t[bk]], signal=(cc == 3))
                    for cc in range(4):
                        kc = half * 4 + cc
                        o = h2T[:, kc, tt * 128:(tt + 1) * 128]
                        i_ = banks[bk][:, cc * 128:(cc + 1) * 128]
                        sc = modB[:, 0, kc, b:b + 1]
                        sh = modB[:, 1, kc, b:b + 1]
                        P.op(act, lambda o=o, i_=i_, sc=sc, sh=sh: nc.scalar.activation(out=o, in_=i_, func=AF.Identity, scale=sc, bias=sh),
                             reads=[bank_t[bk], t_pB], writes=[t_h2T[tt]])

            def layernorm_ops(buf, t_buf, gbc, t_gb, small, t_small):
                ops = []
                for c in range(2):
                    ops.append(lambda c=c: P.op(dve, lambda: nc.vector.bn_stats(out=small[:, c * 6:(c + 1) * 6], in_=buf[:, c * 512:(c + 1) * 512]),
                                                reads=[t_buf], writes=[t_small]))
                ops.append(lambda: P.op(dve, lambda: nc.vector.bn_aggr(out=small[:, 12:14], in_=small[:, 0:12]), reads=[t_small], writes=[t_small]))
                ops.append(lambda: P.op(dve, lambda: nc.vector.tensor_scalar(out=small[:, 14:15], in0=small[:, 13:14], scalar1=EPS, scalar2=None, op0=ALU.add),
                                        reads=[t_small], writes=[t_small]))
                ops.append(lambda: P.op(pool, lambda: nc.gpsimd.tensor_tensor(out=small[:, 15:16], in0=small[:, 14:15], in1=mhalf[:], op=ALU.pow),
                                        reads=[t_small, t_const], writes=[t_small]))
                ops.append(lambda: P.op(dve, lambda: nc.vector.tensor_scalar(out=buf[:], in0=buf[:], scalar1=small[:, 12:13], scalar2=small[:, 15:16],
                                                                             op0=ALU.subtract, op1=ALU.mult), reads=[t_buf, t_small], writes=[t_buf]))
                ops.append(lambda: P.op(dve, lambda: nc.vector.tensor_tensor(out=buf[:], in0=buf[:], in1=gbc, op=ALU.mult),
                                        reads=[t_buf, t_gb], writes=[t_buf]))
                return ops

            def GU(blk, pending):
                h2T, t_h2T = h2T2[blk % 2], t_h2T2[blk % 2]
                for f in range(NF):
                    pr = f % 2
                    bg, bu = 2 * pr, 2 * pr + 1
                    for kc in range(8):
                        P.op(pe, lambda kc=kc: nc.tensor.matmul(banks[bg][:, :], wg[:, kc, f * 128:(f + 1) * 128], h2T[:, kc, :],
                                                                start=(kc == 0), stop=(kc == 7)),
                             reads=[t_wg[f // 2]] + t_h2T, writes=[bank_t[bg]], signal=(kc == 7))
                    for kc in range(8):
                        P.op(pe, lambda kc=kc: nc.tensor.matmul(banks[bu][:, :], wu[:, kc, f * 128:(f + 1) * 128], h2T[:, kc, :],
                                                                start=(kc == 0), stop=(kc == 7)),
                             reads=[t_wu[f // 2]] + t_h2T, writes=[bank_t[bu]], signal=(kc == 7))
                    P.op(act, lambda: nc.scalar.activation(out=sg[pr], in_=banks[bg][:, :], func=AF.Silu),
                         reads=[bank_t[bg]], writes=[t_sg[pr]])
                    P.op(dve, lambda: nc.vector.tensor_tensor(out=gT[:, f, :], in0=banks[bu][:, :], in1=sg[pr], op=ALU.mult),
                         reads=[bank_t[bu], t_sg[pr]], writes=[t_gT[f]])
                    if pending:
                        pending.pop(0)()
                while pending:
                    pending.pop(0)()

            def DN_tile(blk, tt):
                r0 = blk * 512 + tt * 128
                xr, t_xr = next_xb()
                P.dma("sp", xr[:], y_d[r0:r0 + 128, :], writes=[t_xr])
                for half in range(2):
                    bk = 4 + 2 * (tt % 2) + half
                    for f in range(NF):
                        P.op(pe, lambda f=f: nc.tensor.matmul(banks[bk][:, :], gT[:, f, tt * 128:(tt + 1) * 128],
                                                              wd[:, f, half * 512:(half + 1) * 512], start=(f == 0), stop=(f == NF - 1)),
                             reads=[t_wd[f // 2], t_gT[f]], writes=[bank_t[bk]], signal=(f == NF - 1))
                    P.op(dve, lambda half=half: nc.vector.tensor_tensor(out=ytmpB[:, half * 512:(half + 1) * 512], in0=banks[bk][:, :],
                                                                        in1=g2cur[:, half * 512:(half + 1) * 512], op=ALU.mult),
                         reads=[bank_t[bk], t_g2cur], writes=[t_ytmpB])
                ops = [
                    lambda: P.op(dve, lambda: nc.vector.tensor_tensor(out=xr[:], in0=xr[:], in1=agbc[:], op=ALU.mult),
                                 reads=[t_xr, t_pB], writes=[t_xr]),
                    lambda: P.op(dve, lambda: nc.vector.tensor_tensor(out=ytmpB[:], in0=ytmpB[:], in1=abbc[:], op=ALU.add),
                                 reads=[t_ytmpB, t_pB], writes=[t_ytmpB]),
                    lambda: P.op(dve, lambda: nc.vector.tensor_tensor(out=xr[:], in0=xr[:], in1=ytmpB[:], op=ALU.add),
                                 reads=[t_xr, t_ytmpB], writes=[t_xr]),
                ]
                ops += layernorm_ops(xr, t_xr, ln2g[:], t_pB, smallB, t_smallB)
                ops.append(lambda: P.op(dve, lambda: nc.vector.tensor_tensor(out=xr[:], in0=xr[:], in1=ln2b[:], op=ALU.add),
                                        reads=[t_xr, t_pB], writes=[t_xr]))
                ops.append(lambda: P.dma("pool", y_d[r0:r0 + 128, :], xr[:], reads=[t_xr]))
                return ops

            NBLK = NBC * 4
            for tt in range(4):
                TB_tile(0, tt)
            pending = []
            for blk in range(NBLK):
                if blk % 4 == 0:
                    bq = blk // 4
                    P.dma("sp", g2cur[:], g2s_d[bq:bq + 1, :].partition_broadcast(128), reads=[t_g2s], writes=[t_g2cur])
                GU(blk, pending)
                for tt in range(4):
                    ops = DN_tile(blk, tt)
                    if blk + 1 < NBLK:
                        TB_tile(blk + 1, tt)
                    if tt < 3:
                        for o in ops:
                            o()
                    else:
                        pending = ops
            while pending:
                pending.pop(0)()
            P.barrier()
    return nc


def _rope_tables():
    half = 8
    inv_freq = (np.float32(ROPE_THETA) ** (-np.arange(half, dtype=np.float32) * np.float32(2.0) / np.float32(16))).astype(np.float32)
    pos = np.arange(S, dtype=np.float32)
    ang = (pos[:, None] * inv_freq[None, :]).astype(np.float32)
    cos = np.cos(ang).astype(np.float32)
    sin = np.sin(ang).astype(np.float32)

    def lay(t):
        t = t.reshape(16, 128, 1, 8)
        t = np.broadcast_to(t, (16, 128, 8, 8))
        return np.ascontiguousarray(t.transpose(1, 0, 2, 3).reshape(128, 1024))
    return lay(cos), lay(sin)


_NC_CACHE = {}


def kernel(x, c, ada_w, ada_b, w_in, lambda_q1, lambda_k1, lambda_q2, lambda_k2,
           subln_g, sgu_ln_g, sgu_ln_b, sgu_w, sgu_b, w_o, ln1_g, ln1_b,
           w_gate, w_up, w_down, ln2_g, ln2_b):
    f = lambda a: np.ascontiguousarray(np.asarray(a, dtype=np.float32))
    x = f(x)
    c = f(c)
    cos_t, sin_t = _rope_tables()
    ident = np.eye(128, dtype=np.float32)
    tri = np.triu(np.ones((128, 128), dtype=np.float32))
    sel = np.zeros((2, 256), dtype=np.float32)
    sel[0, 0:128] = 1.0
    sel[1, 128:256] = 1.0
    shared = {
        "ada_w": f(ada_w[0]), "ada_b": f(ada_b[0]).reshape(1, -1), "w_in": f(w_in[0]),
        "lam4": f(np.stack([np.asarray(lambda_q1[0]), np.asarray(lambda_k1[0]), np.asarray(lambda_q2[0]), np.asarray(lambda_k2[0])])).reshape(1, 256),
        "subln_g": f(subln_g[0]).reshape(128, 1),
        "sgu_ln_g": f(sgu_ln_g[0]).reshape(1, 512), "sgu_ln_b": f(sgu_ln_b[0]).reshape(1, 512),
        "sgu_wT": f(np.transpose(np.asarray(sgu_w[0]), (0, 2, 1))), "sgu_b": f(sgu_b[0]).reshape(1, 512),
        "w_o": f(w_o[0]), "ln1_g": f(ln1_g[0]).reshape(1, -1), "ln1_b": f(ln1_b[0]).reshape(1, -1),
        "ln1_gT": f(np.asarray(ln1_g[0]).reshape(8, 128).T), "ln1_bT": f(np.asarray(ln1_b[0]).reshape(8, 128).T),
        "w_gate": f(w_gate[0]), "w_up": f(w_up[0]), "w_down": f(w_down[0]),
        "ln2_g": f(ln2_g[0]).reshape(1, -1), "ln2_b": f(ln2_b[0]).reshape(1, -1),
        "ident": ident, "tri": tri, "cos_t": cos_t, "sin_t": sin_t, "sel": sel,
    }
    in_maps = []
    for i in range(NCORES):
        m = dict(shared)
        m["x"] = x[NBC * i:NBC * (i + 1)].reshape(NBC * S, D)
        m["cT"] = np.ascontiguousarray(c[NBC * i:NBC * (i + 1)].T)
        in_maps.append(m)
    if "nc" not in _NC_CACHE:
        _NC_CACHE["nc"] = build_nc()
    nc = _NC_CACHE["nc"]
    res = run_bass_kernel_spmd(nc, in_maps, core_ids=list(range(NCORES)))
    out = np.concatenate([r["y"].reshape(NBC, S, D) for r in res.results], axis=0)
    return out.astype(np.float32)
```

```python
import math
import os
KSTOP = int(os.environ.get('KSTOP', '0'))


class Stop(Exception):
    pass


def chk(n):
    if KSTOP == n:
        raise Stop()
from contextlib import ExitStack
import numpy as np
import concourse.bass as bass
import concourse.mybir as mybir
from concourse.bass_utils import run_bass_kernel_spmd

F32 = mybir.dt.float32
BF16 = mybir.dt.bfloat16
AF = mybir.ActivationFunctionType
ALU = mybir.AluOpType
AX = mybir.AxisListType

D = 1024
S = 2048
NBC = 2
NCORES = 8
DFF = 2816
NF = DFF // 128
PROJ = 2560
ALPHA = 2.0 ** 0.25
EPS = 1e-5
LAM_INIT = 0.2
ROPE_THETA = 500000.0


class Trk:
    __slots__ = ("w", "r", "x")

    def __init__(self, excl=False):
        self.w = None
        self.r = {}
        self.x = excl


class Eng:
    def __init__(self, name, h, sem):
        self.name, self.h, self.sem, self.cnt, self.waited = name, h, sem, 0, {}


class Prog:
    def __init__(self, nc, es):
        self.nc = nc
        mk = lambda n: es.enter_context(nc.semaphore(n))
        self.pe = Eng("pe", nc.tensor, mk("s_pe"))
        self.act = Eng("act", nc.scalar, mk("s_act"))
        self.dve = Eng("dve", nc.vector, mk("s_dve"))
        self.pool = Eng("pool", nc.gpsimd, mk("s_pool"))
        self.sp = Eng("sp", nc.sync, None)
        self.engs = [self.pe, self.act, self.dve, self.pool, self.sp]
        self.sp_sems = [[mk("s_sp%d" % i), 0] for i in range(16)]
        self.pq_sems = [[mk("s_pq%d" % i), 0] for i in range(8)]
        self.sp_i = 0
        self.pq_i = 0
        self.semcnt = {}

    def _emit_waits(self, eng, deps, fn):
        need = []
        for k, (s, v) in deps.items():
            if eng is self.pe and s is self.pe.sem:
                continue
            if eng.waited.get(k, 0) >= v:
                continue
            assert self.semcnt.get(k, (None, 0))[1] >= v, "wait on a signal never emitted (%s)" % eng.name
            need.append((s, v))
            eng.waited[k] = v
        for (s, v) in need[:-1]:
            eng.h.wait_ge(s, v)
        inst = fn()
        if need:
            inst._wait_ge(need[-1][0], need[-1][1])
        return inst

    @staticmethod
    def _deps(reads, writes, extra=(), own=None):
        deps = {}

        def add(ev):
            if ev is None:
                return
            k = id(ev[0])
            if k not in deps or deps[k][1] < ev[1]:
                deps[k] = ev
        for t in reads:
            add(t.w)
            if t.x:
                for ev in t.r.values():
                    if ev[0] is not own:
                        add(ev)
        for t in writes:
            add(t.w)
            for ev in t.r.values():
                add(ev)
        for ev in extra:
            add(ev)
        return deps

    @staticmethod
    def _record(ev, reads, writes):
        for t in writes:
            t.w = ev
            t.r = {}
        for t in reads:
            k = id(ev[0])
            if k not in t.r or t.r[k][1] < ev[1]:
                t.r[k] = ev

    def op(self, eng, fn, reads=(), writes=(), signal=True):
        deps = self._deps(reads, writes, own=eng.sem)
        inst = self._emit_waits(eng, deps, fn)
        if signal:
            eng.cnt += 1
            inst.then_inc(eng.sem, 1)
            self.semcnt[id(eng.sem)] = (eng.sem, eng.cnt)
            ev = (eng.sem, eng.cnt)
        else:
            ev = (eng.sem, eng.cnt + 1)
        self._record(ev, reads, writes)
        return inst

    def dma(self, q, out, in_, reads=(), writes=(), **kw):
        if q == "sp":
            eng, pool = self.sp, self.sp_sems
            i = self.sp_i
            self.sp_i = (i + 1) % len(pool)
        else:
            eng, pool = self.pool, self.pq_sems
            i = self.pq_i
            self.pq_i = (i + 1) % len(pool)
        sem, cnt = pool[i]
        extra = [(sem, cnt)] if cnt > 0 else []
        deps = self._deps(reads, writes, extra)
        inst = self._emit_waits(eng, deps, lambda: eng.h.dma_start(out=out, in_=in_, **kw))
        inst.then_inc(sem, 16)
        pool[i][1] = cnt + 16
        self.semcnt[id(sem)] = (sem, cnt + 16)
        ev = (sem, cnt + 16)
        self._record(ev, reads, writes)

    def all_events(self):
        evs = []
        for e in (self.pe, self.act, self.dve, self.pool):
            if e.cnt > 0:
                evs.append((e.sem, e.cnt))
        for s, c in self.sp_sems + self.pq_sems:
            if c > 0:
                evs.append((s, c))
        return evs

    def barrier(self):
        evs = self.all_events()
        for e in self.engs:
            for (s, v) in evs:
                if e.waited.get(id(s), 0) >= v:
                    continue
                e.h.wait_ge(s, v)
                e.waited[id(s)] = v


def build_nc():
    nc = bass.Bass("TRN2", target_bir_lowering=False)
    dt = lambda n, shp, kind="ExternalInput": nc.dram_tensor(n, shp, F32, kind=kind).ap()
    x_d = dt("x", [NBC * S, D])
    cT_d = dt("cT", [D, NBC])
    adaw_d = dt("ada_w", [D, 6 * D])
    adab_d = dt("ada_b", [1, 6 * D])
    win_d = dt("w_in", [D, PROJ])
    lam_d = dt("lam4", [1, 256])
    subg_d = dt("subln_g", [128, 1])
    sglg_d = dt("sgu_ln_g", [1, 512])
    sglb_d = dt("sgu_ln_b", [1, 512])
    sgwT_d = dt("sgu_wT", [4, 128, 128])
    sgb_d = dt("sgu_b", [1, 512])
    wo_d = dt("w_o", [D, D])
    ln1g_d = dt("ln1_g", [1, D])
    ln1b_d = dt("ln1_b", [1, D])
    ln1gT_d = dt("ln1_gT", [128, 8])
    ln1bT_d = dt("ln1_bT", [128, 8])
    wg_d = dt("w_gate", [D, DFF])
    wu_d = dt("w_up", [D, DFF])
    wd_d = dt("w_down", [DFF, D])
    ln2g_d = dt("ln2_g", [1, D])
    ln2b_d = dt("ln2_b", [1, D])
    ident_d = dt("ident", [128, 128])
    tri_d = dt("tri", [128, 128])
    cos_d = dt("cos_t", [128, 1024])
    sin_d = dt("sin_t", [128, 1024])
    sel_d = dt("sel", [2, 256])
    y_d = dt("y", [NBC * S, D], kind="ExternalOutput")
    g2s_d = dt("g2s", [2, D], kind="ExternalOutput")

    with ExitStack() as es:
        P = Prog(nc, es)
        pe, act, dve, pool = P.pe, P.act, P.dve, P.pool

        def sb(name, shape, dtype, stack=es):
            return stack.enter_context(nc.sbuf_tensor(name, shape, dtype))

        psum_all = es.enter_context(nc.psum_tensor("psum_all", [128, 4096], F32))
        banks = [psum_all[:, i * 512:(i + 1) * 512] for i in range(8)]
        bank_t = [Trk(True) for _ in range(8)]

        ident_f = sb("ident_f", [128, 128], F32)
        epsc = sb("epsc", [128, 1], F32)
        mhalf = sb("mhalf", [128, 1], F32)
        modT = sb("modT", [128, 4, 8, 2], F32)
        t_const = Trk()
        t_modT = Trk()
        t_g2s = Trk()

        P.dma("sp", ident_f[:], ident_d[:, :], writes=[t_const])
        P.op(dve, lambda: nc.vector.memset(epsc[:], EPS), writes=[t_const])
        P.op(dve, lambda: nc.vector.memset(mhalf[:], -0.5), writes=[t_const])

        def evac_copy(i, out, in_, reads, writes):
            if i % 2 == 0:
                P.op(act, lambda: nc.scalar.activation(out=out, in_=in_, func=AF.Copy), reads=reads, writes=writes)
            else:
                P.op(dve, lambda: nc.vector.tensor_copy(out=out, in_=in_), reads=reads, writes=writes)

        def layernorm_inplace(buf, t_buf, width, gbc, bbc, t_gb, small, t_small):
            nchunk = width // 512
            for c in range(nchunk):
                P.op(dve, lambda c=c: nc.vector.bn_stats(out=small[:, c * 6:(c + 1) * 6], in_=buf[:, c * 512:(c + 1) * 512]),
                     reads=[t_buf], writes=[t_small])
            P.op(dve, lambda: nc.vector.bn_aggr(out=small[:, 12:14], in_=small[:, 0:6 * nchunk]), reads=[t_small], writes=[t_small])
            P.op(dve, lambda: nc.vector.tensor_scalar(out=small[:, 14:15], in0=small[:, 13:14], scalar1=EPS, scalar2=None, op0=ALU.add),
                 reads=[t_small], writes=[t_small])
            P.op(pool, lambda: nc.gpsimd.tensor_tensor(out=small[:, 15:16], in0=small[:, 14:15], in1=mhalf[:], op=ALU.pow),
                 reads=[t_small, t_const], writes=[t_small])
            P.op(dve, lambda: nc.vector.tensor_scalar(out=buf[:, 0:width], in0=buf[:, 0:width], scalar1=small[:, 12:13], scalar2=small[:, 15:16],
                                                      op0=ALU.subtract, op1=ALU.mult), reads=[t_buf, t_small], writes=[t_buf])
            if gbc is not None:
                P.op(dve, lambda: nc.vector.tensor_tensor(out=buf[:, 0:width], in0=buf[:, 0:width], in1=gbc, op=ALU.mult),
                     reads=[t_buf, t_gb], writes=[t_buf])

        stopped = [False]
        with ExitStack() as esA:
            try:
                sA = lambda n, shp, d: sb(n, shp, d, esA)
                ident_b = sA("ident_b", [128, 128], BF16)
                tri_f = sA("tri_f", [128, 128], F32)
                tri_b = sA("tri_b", [128, 128], BF16)
                ones_b = sA("ones_b", [128, 128], BF16)
                ones_f = sA("ones_f", [128, 128], F32)
                P.dma("sp", tri_f[:], tri_d[:, :], writes=[t_const])
                P.op(dve, lambda: nc.vector.tensor_copy(out=ident_b[:], in_=ident_f[:]), reads=[t_const], writes=[t_const])
                P.op(dve, lambda: nc.vector.tensor_copy(out=tri_b[:], in_=tri_f[:]), reads=[t_const], writes=[t_const])
                P.op(dve, lambda: nc.vector.memset(ones_b[:], 1.0), writes=[t_const])
                P.op(dve, lambda: nc.vector.memset(ones_f[:], 1.0), writes=[t_const])
                w_in = sA("w_in_sb", [128, 8, PROJ], BF16)
                w_o = sA("w_o_sb", [128, 8, D], BF16)
                t_win = [Trk() for _ in range(8)]
                t_wo = [Trk() for _ in range(8)]
                g1bc = sA("g1bc", [128, 2, 1024], F32)
                t_g1 = Trk()
                sglg = sA("sglg", [128, 512], F32)
                sglb = sA("sglb", [128, 512], F32)
                cos_t = sA("cos_sb", [128, 1024], F32)
                sin_t = sA("sin_sb", [128, 1024], F32)
                WT = sA("WT", [128, 4, 128], BF16)
                sgub = sA("sgub", [1, 512], F32)
                Pt2 = [sA("Pt2_%d" % i, [128, 2, 512], BF16) for i in range(2)]
                t_Pt2 = [Trk(), Trk()]
                Pt = [[Pt2[i][:, c, :] for i in range(2)] for c in range(2)]
                t_Pt = [[t_Pt2[i] for i in range(2)] for c in range(2)]
                tri_b2 = sA("tri_b2", [128, 2, 128], BF16)
                for c_ in range(2):
                    P.op(dve, lambda c_=c_: nc.vector.tensor_copy(out=tri_b2[:, c_, :], in_=tri_f[:]), reads=[t_const], writes=[t_const])
                lam4 = Pt2[0][:, 0, :].bitcast(F32)
                lamt = Pt2[1][:, 0, :].bitcast(F32)[:, 0:128]
                sel = g1bc[0:2, 1, 768:1024]
                lams = sA("lams", [128, 4], F32)
                nlam = sA("nlam", [128, 1], F32)
                gs = sA("gs", [128, 1], F32)
                t_pA = Trk()
                cTs = sA("cTs", [128, 8, 2], F32)
                cact = sA("cact", [128, 8, 2], BF16)
                t_c = Trk()
                t_stage = [Trk() for _ in range(2)]
                fin = [[sA("fin%d_%d" % (j, i), [128, 512], F32) for i in range(4)] for j in range(2)]
                t_fin = [[Trk() for _ in range(4)] for _ in range(2)]
                adab = [fin[1][i][0:1, :] for i in range(2)]
                t_adab = [t_fin[1][i] for i in range(2)]
                modblk = [fin[1][2 + i][0:2, :] for i in range(2)]
                t_modblk = [t_fin[1][2 + i] for i in range(2)]

                NXB = 5
                xbuf = [sA("xbuf%d" % i, [128, D], F32) for i in range(NXB)]
                t_xbuf = [Trk() for _ in range(NXB)]
                xb_i = [0]
                ytmp2 = [sA("ytmp%d" % i, [128, D], F32) for i in range(2)]
                t_ytmp2 = [Trk(), Trk()]
                small = sA("small", [128, 16], F32)
                t_small = Trk()
                small2 = sA("small2", [128, 16], F32)
                t_small2 = Trk()
                hT = sA("hT", [128, 8, 512], BF16)
                t_hT = [Trk() for _ in range(4)]
                uT = sA("uT", [128, 4, 512], BF16)
                t_uT = Trk()
                catT = sA("catT", [128, 8, 512], BF16)
                t_cat_att = [Trk() for _ in range(4)]
                t_cat_sgu = [Trk() for _ in range(4)]
                qT = sA("qT", [128, 4, 512], BF16)
                t_qT = [Trk() for _ in range(4)]
                kT = sA("kT", [128, 4, S], BF16)
                t_kT = [Trk() for _ in range(16)]
                stage = [kT[:, 2 * i:2 * i + 2, :].rearrange("p h (a n) -> p (h a) n", n=512) for i in range(2)]
                V = sA("V", [128, 16, 512], BF16)
                t_V = [Trk() for _ in range(16)]
                qk_tm2 = [sA("qk_tm%d" % i, [128, 1024], BF16) for i in range(2)]
                t_qktm2 = [[Trk(), Trk()] for _ in range(2)]
                rtmp = sA("rtmp", [128, 8, 64], F32)
                t_rtmp = [Trk(), Trk()]
                gsv = sA("gsv", [128, 512], F32)
                t_gsv = Trk()
                vln2 = [sA("vln%d" % i, [128, 512], BF16) for i in range(2)]
                t_vln2 = [Trk(), Trk()]

                chk(1)
                P.dma("sp", cTs[:], cT_d.rearrange("(kc p) b -> p kc b", p=128), writes=[t_c])
                P.dma("sp", sel, sel_d[:, :], writes=[t_g1])
                P.op(act, lambda: nc.scalar.activation(out=cact[:], in_=cTs[:], func=AF.Silu), reads=[t_c], writes=[t_c])

                chk(2)
                win_loaded = [False]

                def load_w_in():
                    for kc in range(8):
                        P.dma("pool", w_in[:, kc, :], win_d[kc * 128:(kc + 1) * 128, :], writes=[t_win[kc]], max_dma_last_dim=4096)

                def mod_block(j, sidx, stage_kc, t_stg, load_stage, bk, tbk, bbk):
                    load_stage(j)
                    P.dma("sp", adab[sidx], adab_d[0:1, j * 512:(j + 1) * 512], writes=[t_adab[sidx]])
                    for kc in range(8):
                        P.op(pe, lambda kc=kc: nc.tensor.matmul(banks[bk][0:2, :], cact[:, kc, :], stage_kc(kc),
                                                                start=(kc == 0), stop=False),
                             reads=[t_c] + t_stg, writes=[bank_t[bk]], signal=False)
                    P.op(pe, lambda: nc.tensor.matmul(banks[bk][0:2, :], ones_f[0:1, 0:2], adab[sidx], start=False, stop=True),
                         reads=[t_const, t_adab[sidx]], writes=[bank_t[bk]])
                    P.op(dve, lambda: nc.vector.tensor_copy(out=modblk[sidx], in_=banks[bk][0:2, :]),
                         reads=[bank_t[bk]], writes=[t_modblk[sidx]])
                    gi, hh = j // 2, j % 2
                    if gi in (0, 1, 3, 4):
                        grp = {0: 0, 1: 1, 3: 2, 4: 3}[gi]
                        for cc in range(4):
                            P.op(pe, lambda cc=cc: nc.tensor.matmul(banks[tbk][:, cc * 2:cc * 2 + 2], modblk[sidx][:, cc * 128:(cc + 1) * 128],
                                                                    ident_f[0:2, 0:2], start=True, stop=True),
                                 reads=[t_modblk[sidx], t_const], writes=[bank_t[tbk]], signal=(cc == 3))
                        src = banks[tbk][:, 0:8].rearrange("p (c b) -> p c b", b=2)
                        dst = modT[:, grp, hh * 4:(hh + 1) * 4, :]
                        if gi in (1, 4):
                            P.op(dve, lambda: nc.vector.tensor_scalar(out=dst, in0=src, scalar1=1.0, scalar2=None, op0=ALU.add),
                                 reads=[bank_t[tbk]], writes=[t_modT])
                        else:
                            P.op(dve, lambda: nc.vector.tensor_copy(out=dst, in_=src), reads=[bank_t[tbk]], writes=[t_modT])
                    elif gi == 2:
                        for b in range(2):
                            xbk = (tbk, bbk)[b]
                            P.op(pe, lambda b=b: nc.tensor.matmul(banks[xbk][:, :], sel[0:2, b * 128:(b + 1) * 128], modblk[sidx],
                                                                  start=True, stop=True),
                                 reads=[t_pA, t_modblk[sidx], t_g1], writes=[bank_t[xbk]])
                            P.op(act, lambda b=b: nc.scalar.activation(out=g1bc[:, b, hh * 512:(hh + 1) * 512], in_=banks[xbk][:, :], func=AF.Copy),
                                 reads=[bank_t[xbk]], writes=[t_g1])
                    else:
                        P.dma("sp", g2s_d[0:2, hh * 512:(hh + 1) * 512], modblk[sidx], reads=[t_modblk[sidx]], writes=[t_g2s])

                for jj, j in enumerate([2, 3, 0, 1]):
                    sidx = jj % 2

                    def ld(j, sidx=sidx):
                        P.dma("pool", stage[sidx], adaw_d[:, j * 512:(j + 1) * 512].rearrange("(kc p) n -> p kc n", p=128),
                              writes=[t_stage[sidx]])
                    mod_block(j, sidx, lambda kc, sidx=sidx: stage[sidx][:, kc, :], [t_stage[sidx]], ld, jj % 2, 2 + (jj % 2), 4 + (jj % 2))
                    if jj == 1:
                        load_w_in()

                chk(4)
                for t in t_kT:
                    for ts in t_stage:
                        for k, ev in ts.r.items():
                            if k not in t.r or t.r[k][1] < ev[1]:
                                t.r[k] = ev
                        if ts.w is not None and (t.w is None or True):
                            t.r[id(ts.w[0])] = ts.w

                chk(5)
                P.dma("sp", cos_t[:], cos_d[:, :], writes=[t_pA])
                P.dma("sp", sin_t[:], sin_d[:, :], writes=[t_pA])
                P.dma("sp", sglg[:], sglg_d[0:1, :].partition_broadcast(128), writes=[t_pA])
                P.dma("sp", sglb[:], sglb_d[0:1, :].partition_broadcast(128), writes=[t_pA])
                P.dma("sp", sgub[:], sgb_d[0:1, :], writes=[t_pA])
                WTf = gsv[:].rearrange("p (g t) -> p g t", t=128)
                P.dma("sp", WTf, sgwT_d.rearrange("g s t -> s g t"), writes=[t_gsv])
                P.dma("sp", lam4, lam_d[0:1, :].partition_broadcast(128), writes=[t_pA, t_Pt[0][0]])
                P.dma("sp", gs[:], subg_d[:, :], writes=[t_pA])
                for g in range(4):
                    P.op(dve, lambda g=g: nc.vector.tensor_tensor(out=WT[:, g, :], in0=WTf[:, g, :], in1=tri_f[:], op=ALU.mult),
                         reads=[t_pA, t_const, t_gsv], writes=[t_pA])
                P.op(dve, lambda: nc.vector.tensor_scalar(out=gs[:], in0=gs[:], scalar1=1.0 - LAM_INIT, scalar2=None, op0=ALU.mult),
                     reads=[t_pA], writes=[t_pA])
                P.op(dve, lambda: nc.vector.tensor_tensor(out=lamt[:, 0:64], in0=lam4[:, 0:64], in1=lam4[:, 64:128], op=ALU.mult),
                     reads=[t_pA, t_Pt[0][0]], writes=[t_pA, t_Pt[0][1]])
                P.op(dve, lambda: nc.vector.tensor_tensor(out=lamt[:, 64:128], in0=lam4[:, 128:192], in1=lam4[:, 192:256], op=ALU.mult),
                     reads=[t_pA, t_Pt[0][0]], writes=[t_pA, t_Pt[0][1]])
                P.op(dve, lambda: nc.vector.tensor_reduce(out=lams[:, 0:2], in_=lamt.rearrange("p (a d) -> p a d", d=64), axis=AX.X, op=ALU.add), reads=[t_pA, t_Pt[0][1]], writes=[t_pA])
                P.op(act, lambda: nc.scalar.activation(out=lams[:, 2:4], in_=lams[:, 0:2], func=AF.Exp), reads=[t_pA], writes=[t_pA])
                P.op(dve, lambda: nc.vector.tensor_tensor(out=nlam[:], in0=lams[:, 3:4], in1=lams[:, 2:3], op=ALU.subtract),
                     reads=[t_pA], writes=[t_pA])
                P.op(dve, lambda: nc.vector.tensor_scalar(out=nlam[:], in0=nlam[:], scalar1=-LAM_INIT, scalar2=None, op0=ALU.add),
                     reads=[t_pA], writes=[t_pA])

                chk(6)
                mm_i = [0]
                tr_i = [0]
                a1_i = [0]

                def next_mm():
                    mm_i[0] = (mm_i[0] + 1) % 4
                    return mm_i[0]

                def next_tr():
                    tr_i[0] ^= 1
                    return 6 + tr_i[0]

                def rope_evac(bk, which, pos_tile, par):
                    qk_tm, t_qktm = qk_tm2[par], t_qktm2[par]
                    src = banks[bk][:, :].rearrange("p (c d) -> p c d", d=64)
                    dst = qk_tm[:, which * 512:(which + 1) * 512].rearrange("p (c d) -> p c d", d=64)
                    cs = cos_t[:, pos_tile * 64:(pos_tile + 1) * 64].rearrange("p (c d) -> p c d", d=8)
                    sn = sin_t[:, pos_tile * 64:(pos_tile + 1) * 64].rearrange("p (c d) -> p c d", d=8)
                    t1, t2 = src[:, :, 0:8], src[:, :, 8:16]
                    r = [rtmp[:, which * 4 + i, :].rearrange("p (c d) -> p c d", d=8) for i in range(4)]
                    tq = t_qktm[which]
                    trt = t_rtmp[which]
                    P.op(act, lambda: nc.scalar.activation(out=dst[:, :, 16:64], in_=src[:, :, 16:64], func=AF.Copy),
                         reads=[bank_t[bk]], writes=[tq])
                    P.op(dve, lambda: nc.vector.tensor_tensor(out=r[0], in0=t1, in1=cs, op=ALU.mult), reads=[bank_t[bk], t_pA], writes=[trt])
                    P.op(dve, lambda: nc.vector.tensor_tensor(out=r[1], in0=t2, in1=sn, op=ALU.mult), reads=[bank_t[bk], t_pA], writes=[trt])
                    P.op(dve, lambda: nc.vector.tensor_tensor(out=r[2], in0=t1, in1=sn, op=ALU.mult), reads=[bank_t[bk], t_pA], writes=[trt])
                    P.op(dve, lambda: nc.vector.tensor_tensor(out=r[3], in0=t2, in1=cs, op=ALU.mult), reads=[bank_t[bk], t_pA], writes=[trt])
                    P.op(dve, lambda: nc.vector.tensor_tensor(out=dst[:, :, 0:8], in0=r[0], in1=r[1], op=ALU.subtract), reads=[trt], writes=[tq])
                    P.op(dve, lambda: nc.vector.tensor_tensor(out=dst[:, :, 8:16], in0=r[2], in1=r[3], op=ALU.add), reads=[trt], writes=[tq])

                def A1_tile(gb, tt):
                    b, tb = divmod(gb, 4)
                    xi = xb_i[0]
                    xb_i[0] = (xi + 1) % NXB
                    r0 = gb * 512 + tt * 128
                    xt, t_xt = xbuf[xi], t_xbuf[xi]
                    P.dma("sp", xt[:], x_d[r0:r0 + 128, :], writes=[t_xt])
                    for half in range(2):
                        a1_i[0] = (a1_i[0] + 1) % 4
                        bk = 4 + a1_i[0]
                        for cc in range(4):
                            kc = half * 4 + cc
                            P.op(pe, lambda cc=cc, kc=kc: nc.tensor.transpose(banks[bk][:, cc * 128:(cc + 1) * 128],
                                                                              xt[:, kc * 128:(kc + 1) * 128], ident_f[:]),
                                 reads=[t_xt, t_const], writes=[bank_t[bk]], signal=(cc == 3))
                        for cc in range(4):
                            kc = half * 4 + cc
                            o = hT[:, kc, tt * 128:(tt + 1) * 128]
                            i_ = banks[bk][:, cc * 128:(cc + 1) * 128]
                            sc = modT[:, 1, kc, b:b + 1]
                            sh = modT[:, 0, kc, b:b + 1]
                            if bk % 2 == 0:
                                P.op(act, lambda o=o, i_=i_, sc=sc, sh=sh: nc.scalar.activation(out=o, in_=i_, func=AF.Identity, scale=sc, bias=sh),
                                     reads=[bank_t[bk], t_modT], writes=[t_hT[tt]])
                            else:
                                P.op(dve, lambda o=o, i_=i_, sc=sc, sh=sh: nc.vector.tensor_scalar(out=o, in0=i_, scalar1=sc, scalar2=sh,
                                                                                                    op0=ALU.mult, op1=ALU.add),
                                     reads=[bank_t[bk], t_modT], writes=[t_hT[tt]])

                def A2a(gb):
                    for j in range(4):
                        bk = next_mm()
                        for kc in range(8):
                            P.op(pe, lambda kc=kc: nc.tensor.matmul(banks[bk][:, :], w_in[:, kc, 1536 + j * 128:1536 + (j + 1) * 128],
                                                                    hT[:, kc, :], start=(kc == 0), stop=(kc == 7)),
                                 reads=[t_win[kc]] + t_hT, writes=[bank_t[bk]], signal=(kc == 7))
                        P.op(act, lambda: nc.scalar.activation(out=uT[:, j, :], in_=banks[bk][:, :], func=AF.Gelu),
                             reads=[bank_t[bk]], writes=[t_uT])

                def A2b_mm(gb, tt):
                    b, tb = divmod(gb, 4)
                    ptile = tb * 4 + tt
                    vln, t_vln = vln2[tt % 2], t_vln2[tt % 2]
                    for grp, col0 in (("q", 0), ("k", 512), ("v", 1024), ("sv", 2048)):
                        bk = next_mm()
                        for kc in range(8):
                            P.op(pe, lambda kc=kc: nc.tensor.matmul(banks[bk][:, :], hT[:, kc, tt * 128:(tt + 1) * 128],
                                                                    w_in[:, kc, col0:col0 + 512], start=(kc == 0), stop=(kc == 7)),
                                 reads=[t_win[kc], t_hT[tt]], writes=[bank_t[bk]], signal=(kc == 7))
                        if grp == "q":
                            rope_evac(bk, 0, ptile, tt % 2)
                        elif grp == "k":
                            rope_evac(bk, 1, ptile, tt % 2)
                        elif grp == "v":
                            P.op(act, lambda: nc.scalar.activation(out=V[:, ptile, :], in_=banks[bk][:, :], func=AF.Copy), reads=[bank_t[bk]], writes=[t_V[ptile]])
                        else:
                            P.op(act, lambda: nc.scalar.activation(out=gsv[:], in_=banks[bk][:, :], func=AF.Gelu),
                                 reads=[bank_t[bk]], writes=[t_gsv])
                            P.op(dve, lambda: nc.vector.bn_stats(out=small[:, 0:6], in_=gsv[:]), reads=[t_gsv], writes=[t_small])
                            P.op(dve, lambda: nc.vector.bn_aggr(out=small[:, 12:14], in_=small[:, 0:6]), reads=[t_small], writes=[t_small])
                            P.op(dve, lambda: nc.vector.tensor_scalar(out=small[:, 14:15], in0=small[:, 13:14], scalar1=EPS, scalar2=None,
                                                                      op0=ALU.add), reads=[t_small], writes=[t_small])
                            P.op(pool, lambda: nc.gpsimd.tensor_tensor(out=small[:, 15:16], in0=small[:, 14:15], in1=mhalf[:], op=ALU.pow),
                                 reads=[t_small, t_const], writes=[t_small])
                            P.op(dve, lambda: nc.vector.tensor_scalar(out=gsv[:], in0=gsv[:], scalar1=small[:, 12:13], scalar2=small[:, 15:16],
                                                                      op0=ALU.subtract, op1=ALU.mult), reads=[t_gsv, t_small], writes=[t_gsv])
                            P.op(dve, lambda: nc.vector.tensor_tensor(out=gsv[:], in0=gsv[:], in1=sglg[:], op=ALU.mult),
                                 reads=[t_gsv, t_pA], writes=[t_gsv])
                            P.op(dve, lambda: nc.vector.tensor_tensor(out=vln[:], in0=gsv[:], in1=sglb[:], op=ALU.add),
                                 reads=[t_gsv, t_pA], writes=[t_vln])

                def A2b_dep(gb, tt):
                    b, tb = divmod(gb, 4)
                    ptile = tb * 4 + tt
                    vln, t_vln = vln2[tt % 2], t_vln2[tt % 2]
                    qk_tm, t_qktm = qk_tm2[tt % 2], t_qktm2[tt % 2]
                    for which in range(2):
                        tbk = next_tr()
                        bfv = banks[tbk][:, :].bitcast(BF16)
                        for h in range(4):
                            P.op(pe, lambda h=h: nc.tensor.transpose(bfv[:, h * 128:(h + 1) * 128],
                                                                     qk_tm[:, which * 512 + h * 128:which * 512 + (h + 1) * 128], ident_b[:]),
                                 reads=[t_qktm[which], t_const], writes=[bank_t[tbk]], signal=(h == 3))
                        srcv = bfv[:, 0:512].rearrange("p (h t) -> p h t", t=128)
                        if which == 0:
                            P.op(act, lambda: nc.scalar.activation(out=qT[:, :, tt * 128:(tt + 1) * 128], in_=srcv, func=AF.Copy),
                                 reads=[bank_t[tbk]], writes=[t_qT[tt]])
                        else:
                            P.op(act, lambda: nc.scalar.activation(out=kT[:, :, ptile * 128:(ptile + 1) * 128], in_=srcv, func=AF.Copy),
                                 reads=[bank_t[tbk]], writes=[t_kT[ptile]])
                    sbk = next_tr()
                    for g in range(4):
                        P.op(pe, lambda g=g: nc.tensor.matmul(banks[sbk][:, g * 128:(g + 1) * 128], vln[:, g * 128:(g + 1) * 128],
                                                              WT[:, g, :], start=True, stop=False),
                             reads=[t_vln, t_pA], writes=[bank_t[sbk]], signal=False)
                        P.op(pe, lambda g=g: nc.tensor.matmul(banks[sbk][:, g * 128:(g + 1) * 128], ones_f[0:1, :],
                                                              sgub[0:1, g * 128:(g + 1) * 128], start=False, stop=True),
                             reads=[t_const, t_pA], writes=[bank_t[sbk]], signal=(g == 3))
                    P.op(dve, lambda: nc.vector.tensor_tensor(out=catT[:, 4:8, tt * 128:(tt + 1) * 128],
                                                              in0=banks[sbk][:, :].rearrange("p (g t) -> p g t", t=128),
                                                              in1=uT[:, :, tt * 128:(tt + 1) * 128], op=ALU.mult),
                         reads=[bank_t[sbk], t_uT], writes=[t_cat_sgu[tt]])

                def fin_part1(h):
                    fs, tf = fin[h % 2], t_fin[h % 2]
                    P.op(dve, lambda: nc.vector.tensor_copy(out=fs[0][:], in_=banks[4][:, :]), reads=[bank_t[4]], writes=[tf[0]])
                    P.op(act, lambda: nc.scalar.activation(out=fs[2][:], in_=banks[6][:, :], func=AF.Ln), reads=[bank_t[6]], writes=[tf[2]])
                    P.op(dve, lambda: nc.vector.tensor_copy(out=fs[1][:], in_=banks[5][:, :]), reads=[bank_t[5]], writes=[tf[1]])
                    P.op(act, lambda: nc.scalar.activation(out=fs[3][:], in_=banks[7][:, :], func=AF.Ln), reads=[bank_t[7]], writes=[tf[3]])

                def fin_part2_ops(h):
                    fs, tf = fin[h % 2], t_fin[h % 2]
                    ops = []
                    for c in range(2):
                        ops.append(lambda bkf, c=c: P.op(act, lambda: nc.scalar.activation(out=fs[2 + c][:], in_=fs[2 + c][:], func=AF.Exp, scale=-1.0),
                                                        reads=[tf[2 + c]], writes=[tf[2 + c]]))
                        ops.append(lambda bkf, c=c: P.op(dve, lambda: nc.vector.tensor_tensor(out=fs[c][:], in0=fs[c][:], in1=fs[2 + c][:], op=ALU.mult),
                                                        reads=[tf[c], tf[2 + c]], writes=[tf[c]]))
                    ops.append(lambda bkf: P.op(dve, lambda: nc.vector.scalar_tensor_tensor(out=fs[0][:], in0=fs[1][:], scalar=nlam[:, 0:1], in1=fs[0][:],
                                                                                            op0=ALU.mult, op1=ALU.add),
                                                reads=[tf[1], t_pA], writes=[tf[0]]))
                    ops.append(lambda bkf: P.op(dve, lambda: nc.vector.tensor_tensor(out=fs[2][:], in0=fs[0][:], in1=fs[0][:], op=ALU.mult),
                                                reads=[tf[0]], writes=[tf[2]]))

                    def rms(bkf):
                        P.op(pe, lambda: nc.tensor.matmul(banks[bkf][:, :], ones_f[:], fs[2][:], start=True, stop=True),
                             reads=[t_const, tf[2]], writes=[bank_t[bkf]])
                        P.op(act, lambda: nc.scalar.activation(out=fs[3][:], in_=banks[bkf][:, :], func=AF.Ln,
                                                               scale=1.0 / 128.0, bias=epsc[:, 0:1]),
                             reads=[bank_t[bkf], t_const], writes=[tf[3]])
                        P.op(act, lambda: nc.scalar.activation(out=fs[3][:], in_=fs[3][:], func=AF.Exp, scale=-0.5),
                             reads=[tf[3]], writes=[tf[3]])
                    ops.append(rms)
                    ops.append(lambda bkf: P.op(dve, lambda: nc.vector.scalar_tensor_tensor(out=catT[:, h, :], in0=fs[0][:], scalar=gs[:, 0:1], in1=fs[3][:],
                                                                                            op0=ALU.mult, op1=ALU.mult),
                                                reads=[tf[0], tf[3], t_pA], writes=[t_cat_att[h]]))
                    return ops

                def A3(gb):
                    b, tb = divmod(gb, 4)
                    nkt = 4 * tb + 4
                    steps = [(h, kt) for h in range(4) for kt in range(nkt)]

                    def q0_of(kt):
                        jd = kt - 4 * tb
                        return 0 if jd <= 0 else jd * 128

                    def S(i):
                        h, kt = steps[i]
                        q0 = q0_of(kt)
                        sb0 = (i % 2) * 2
                        for c in range(2):
                            lo, hi = c * 64, (c + 1) * 64
                            P.op(pe, lambda c=c, lo=lo, hi=hi: nc.tensor.matmul(banks[sb0 + c][:, q0:512], kT[lo:hi, h, kt * 128:(kt + 1) * 128],
                                                                                qT[lo:hi, h, q0:512], start=True, stop=True),
                                 reads=[t_kT[kt]] + t_qT, writes=[bank_t[sb0 + c]])

                    pending = []
                    S(0)
                    for i, (h, kt) in enumerate(steps):
                        if i + 1 < len(steps):
                            S(i + 1)
                        jd = kt - 4 * tb
                        q0 = q0_of(kt)
                        pi = i % 2
                        sb0 = pi * 2
                        s2 = psum_all[:, sb0 * 512:(sb0 + 2) * 512].rearrange("p (c n) -> p c n", n=512)
                        P.op(act, lambda: nc.scalar.activation(out=Pt2[pi][:, :, q0:512], in_=s2[:, :, q0:512], func=AF.Exp, scale=0.125),
                             reads=[bank_t[sb0], bank_t[sb0 + 1]], writes=[t_Pt2[pi]])
                        if jd >= 0:
                            P.op(dve, lambda: nc.vector.tensor_tensor(out=Pt2[pi][:, :, q0:q0 + 128], in0=Pt2[pi][:, :, q0:q0 + 128],
                                                                      in1=tri_b2[:], op=ALU.mult),
                                 reads=[t_Pt2[pi], t_const], writes=[t_Pt2[pi]])
                        for c in range(2):
                            P.op(pe, lambda c=c: nc.tensor.matmul(banks[4 + c][:, q0:512], V[:, kt, h * 128:(h + 1) * 128], Pt[c][pi][:, q0:512],
                                                                  start=(kt == 0), stop=(kt == nkt - 1)),
                                 reads=[t_V[kt], t_Pt[c][pi]], writes=[bank_t[4 + c]], signal=False)
                            P.op(pe, lambda c=c: nc.tensor.matmul(banks[6 + c][:, q0:512], ones_b[:], Pt[c][pi][:, q0:512],
                                                                  start=(kt == 0), stop=(kt == nkt - 1)),
                                 reads=[t_const, t_Pt[c][pi]], writes=[bank_t[6 + c]], signal=True)
                        if kt >= 1:
                            for _ in range(3):
                                if pending:
                                    pending.pop(0)(sb0)
                        if kt == nkt - 1:
                            while pending:
                                pending.pop(0)(sb0)
                            fin_part1(h)
                            pending = fin_part2_ops(h)
                    while pending:
                        pending.pop(0)(2)

                def A4_tile(gb, tt):
                    b, tb = divmod(gb, 4)
                    ytmp, t_ytmp = ytmp2[tt % 2], t_ytmp2[tt % 2]
                    xi = xb_i[0]
                    xb_i[0] = (xi + 1) % NXB
                    r0 = gb * 512 + tt * 128
                    xr, t_xr = xbuf[xi], t_xbuf[xi]
                    P.dma("sp", xr[:], x_d[r0:r0 + 128, :], writes=[t_xr])
                    for half in range(2):
                        bk = next_mm()
                        for kc in range(8):
                            rd = [t_wo[kc], t_cat_att[kc]] if kc < 4 else [t_wo[kc], t_cat_sgu[tt]]
                            P.op(pe, lambda kc=kc: nc.tensor.matmul(banks[bk][:, :], catT[:, kc, tt * 128:(tt + 1) * 128],
                                                                    w_o[:, kc, half * 512:(half + 1) * 512], start=(kc == 0), stop=(kc == 7)),
                                 reads=rd, writes=[bank_t[bk]], signal=(kc == 7))
                        P.op(dve, lambda half=half: nc.vector.tensor_tensor(out=ytmp[:, half * 512:(half + 1) * 512], in0=banks[bk][:, :],
                                                                            in1=g1bc[:, b, half * 512:(half + 1) * 512], op=ALU.mult),
                             reads=[bank_t[bk], t_g1], writes=[t_ytmp])
                    P.op(dve, lambda: nc.vector.scalar_tensor_tensor(out=xr[:], in0=xr[:], scalar=ALPHA, in1=ytmp[:], op0=ALU.mult, op1=ALU.add),
                         reads=[t_xr, t_ytmp], writes=[t_xr])
                    layernorm_inplace(xr, t_xr, D, None, None, t_pA, small2, t_small2)
                    P.dma("pool", y_d[r0:r0 + 128, :], xr[:], reads=[t_xr])

                deferred = [4, 5]

                def deferred_load(j):
                    for i in range(4):
                        P.dma("pool", fin[0][i][:].bitcast(BF16).rearrange("p (a n) -> p a n", n=512),
                              adaw_d[i * 256:(i + 1) * 256, j * 512:(j + 1) * 512].rearrange("(a p) n -> p a n", p=128),
                              writes=[t_fin[0][i]])

                def deferred_compute(j):
                    mod_block(j, 0, lambda kc: fin[0][kc // 2][:].bitcast(BF16)[:, (kc % 2) * 512:(kc % 2 + 1) * 512],
                              [t_fin[0][i] for i in range(4)], lambda j: None, 4, 5, 4)

                def emit_deferred():
                    if not deferred:
                        return
                    j = deferred.pop(0)
                    deferred_load(j)
                    deferred_compute(j)

                late = {1: 10, 2: 11, 3: 6, 4: 7, 5: 8, 6: 9}

                NGB = NBC * 4
                for gb in range(NGB):
                    prev = gb - 1
                    if gb in late:
                        deferred_load(late[gb])
                    for tt in range(4):
                        A1_tile(gb, tt)
                    chk(7)
                    if gb > 0:
                        A4_tile(prev, 0)
                    A2a(gb)
                    if gb == 0:
                        for kc in range(8):
                            P.dma("pool", w_o[:, kc, :], wo_d[kc * 128:(kc + 1) * 128, :], writes=[t_wo[kc]])
                    emit_deferred()
                    chk(8)
                    if gb > 0:
                        A4_tile(prev, 1)
                    A2b_mm(gb, 0)
                    emit_deferred()
                    if gb > 0:
                        A4_tile(prev, 2)
                    A2b_mm(gb, 1)
                    A2b_dep(gb, 0)
                    emit_deferred()
                    if gb > 0:
                        A4_tile(prev, 3)
                    A2b_mm(gb, 2)
                    A2b_dep(gb, 1)
                    emit_deferred()
                    A2b_mm(gb, 3)
                    A2b_dep(gb, 2)
                    A2b_dep(gb, 3)
                    if gb in late:
                        deferred_compute(late[gb])
                    chk(9)
                    A3(gb)
                    chk(10)
                for tt in range(4):
                    A4_tile(NGB - 1, tt)
            except Stop:
                stopped[0] = True
            P.barrier()

        with ExitStack() as esB:
          if not stopped[0]:
            sB = lambda n, shp, d: sb(n, shp, d, esB)
            wg = sB("wg_sb", [128, 8, DFF], BF16)
            wu = sB("wu_sb", [128, 8, DFF], BF16)
            wd = sB("wd_sb", [128, NF, D], BF16)
            t_wg = [Trk() for _ in range(11)]
            t_wu = [Trk() for _ in range(11)]
            t_wd = [Trk() for _ in range(11)]
            ln2g = sB("ln2g", [128, D], F32)
            ln2b = sB("ln2b", [128, D], F32)
            t_pB = Trk()
            h2T2 = [sB("h2T%d" % i, [128, 8, 512], BF16) for i in range(2)]
            t_h2T2 = [[Trk() for _ in range(4)] for _ in range(2)]
            gT = sB("gT", [128, NF, 512], BF16)
            t_gT = [Trk() for _ in range(NF)]
            NXBB = 2
            xbB = [sB("xbB_%d" % i, [128, D], F32) for i in range(NXBB)]
            t_xbB = [Trk() for _ in range(NXBB)]
            xbB_i = [0]
            ytmpB = sB("ytmpB", [128, D], F32)
            t_ytmpB = Trk()
            sgy = sB("sgy", [128, D], F32)
            sg = [sgy[:, 0:512], sgy[:, 512:1024]]
            t_sg = [Trk(), Trk()]
            g2cur = sB("g2cur", [128, D], F32)
            t_g2cur = Trk()
            agbc = sB("agbc", [128, D], F32)
            abbc = sB("abbc", [128, D], F32)
            gfm = sB("gfm", [128, 8], F32)
            bfm = sB("bfm", [128, 8], F32)
            modB = sB("modB", [128, 2, 8, 2], F32)
            smallB = sB("smallB", [128, 16], F32)
            t_smallB = Trk()

            P.dma("sp", ln2g[:], ln2g_d[0:1, :].partition_broadcast(128), writes=[t_pB])
            P.dma("sp", ln2b[:], ln2b_d[0:1, :].partition_broadcast(128), writes=[t_pB])
            P.dma("sp", agbc[:], ln1g_d[0:1, :].partition_broadcast(128), writes=[t_pB])
            P.dma("sp", abbc[:], ln1b_d[0:1, :].partition_broadcast(128), writes=[t_pB])
            P.dma("sp", gfm[:], ln1gT_d[:, :], writes=[t_pB])
            P.dma("sp", bfm[:], ln1bT_d[:, :], writes=[t_pB])
            P.op(dve, lambda: nc.vector.tensor_scalar(out=agbc[:], in0=agbc[:], scalar1=ALPHA, scalar2=None, op0=ALU.mult), reads=[t_pB], writes=[t_pB])
            P.op(dve, lambda: nc.vector.tensor_scalar(out=abbc[:], in0=abbc[:], scalar1=ALPHA, scalar2=None, op0=ALU.mult), reads=[t_pB], writes=[t_pB])
            for b_ in range(2):
                P.op(dve, lambda b_=b_: nc.vector.tensor_tensor(out=modB[:, 0, :, b_], in0=modT[:, 3, :, b_], in1=gfm[:], op=ALU.mult),
                     reads=[t_modT, t_pB], writes=[t_pB])
                P.op(dve, lambda b_=b_: nc.vector.tensor_tensor(out=modB[:, 1, :, b_], in0=modT[:, 3, :, b_], in1=bfm[:], op=ALU.mult),
                     reads=[t_modT, t_pB], writes=[t_pB])
                P.op(dve, lambda b_=b_: nc.vector.tensor_tensor(out=modB[:, 1, :, b_], in0=modB[:, 1, :, b_], in1=modT[:, 2, :, b_], op=ALU.add),
                     reads=[t_modT, t_pB], writes=[t_pB])
            wg_v = wg_d.rearrange("(kc p) n -> p kc n", p=128)
            wu_v = wu_d.rearrange("(kc p) n -> p kc n", p=128)
            wd_v = wd_d.rearrange("(f p) n -> p f n", p=128)
            for cgi in range(11):
                c0, c1 = cgi * 256, (cgi + 1) * 256
                P.dma("pool", wg[:, :, c0:c1], wg_v[:, :, c0:c1], writes=[t_wg[cgi]])
                P.dma("pool", wu[:, :, c0:c1], wu_v[:, :, c0:c1], writes=[t_wu[cgi]])
            for cgi in range(11):
                P.dma("pool", wd[:, 2 * cgi:2 * cgi + 2, :], wd_v[:, 2 * cgi:2 * cgi + 2, :], writes=[t_wd[cgi]])

            tbB = [0]

            def next_xb():
                i = xbB_i[0]
                xbB_i[0] = (i + 1) % NXBB
                return xbB[i], t_xbB[i]

            def TB_tile(blk, tt):
                b = blk // 4
                h2T, t_h2T = h2T2[blk % 2], t_h2T2[blk % 2]
                r0 = blk * 512 + tt * 128
                xt = sgy
                P.dma("sp", xt[:], y_d[r0:r0 + 128, :], writes=t_sg)
                for half in range(2):
                    tbB[0] = (tbB[0] + 1) % 4
                    bk = tbB[0]
                    for cc in range(4):
                        kc = half * 4 + cc
                        P.op(pe, lambda cc=cc, kc=kc: nc.tensor.transpose(banks[bk][:, cc * 128:(cc + 1) * 128],
                                                                          xt[:, kc * 128:(kc + 1) * 128], ident_f[:]),
                             reads=t_sg + [t_const], writes=[bank_t[bk]], signal=(cc == 3))
                    for cc in range(4):
                        kc = half * 4 + cc
                        o = h2T[:, kc, tt * 128:(tt + 1) * 128]
                        i_ = banks[bk][:, cc * 128:(cc + 1) * 128]
                        sc = modB[:, 0, kc, b:b + 1]
                        sh = modB[:, 1, kc, b:b + 1]
                        P.op(act, lambda o=o, i_=i_, sc=sc, sh=sh: nc.scalar.activation(out=o, in_=i_, func=AF.Identity, scale=sc, bias=sh),
                             reads=[bank_t[bk], t_pB], writes=[t_h2T[tt]])

            def layernorm_ops(buf, t_buf, gbc, t_gb, small, t_small):
                ops = []
                for c in range(2):
                    ops.append(lambda c=c: P.op(dve, lambda: nc.vector.bn_stats(out=small[:, c * 6:(c + 1) * 6], in_=buf[:, c * 512:(c + 1) * 512]),
                                                reads=[t_buf], writes=[t_small]))
                ops.append(lambda: P.op(dve, lambda: nc.vector.bn_aggr(out=small[:, 12:14], in_=small[:, 0:12]), reads=[t_small], writes=[t_small]))
                ops.append(lambda: P.op(dve, lambda: nc.vector.tensor_scalar(out=small[:, 14:15], in0=small[:, 13:14], scalar1=EPS, scalar2=None, op0=ALU.add),
                                        reads=[t_small], writes=[t_small]))
                ops.append(lambda: P.op(pool, lambda: nc.gpsimd.tensor_tensor(out=small[:, 15:16], in0=small[:, 14:15], in1=mhalf[:], op=ALU.pow),
                                        reads=[t_small, t_const], writes=[t_small]))
                ops.append(lambda: P.op(dve, lambda: nc.vector.tensor_scalar(out=buf[:], in0=buf[:], scalar1=small[:, 12:13], scalar2=small[:, 15:16],
                                                                             op0=ALU.subtract, op1=ALU.mult), reads=[t_buf, t_small], writes=[t_buf]))
                ops.append(lambda: P.op(dve, lambda: nc.vector.tensor_tensor(out=buf[:], in0=buf[:], in1=gbc, op=ALU.mult),
                                        reads=[t_buf, t_gb], writes=[t_buf]))
                return ops

            def GU(blk, pending):
                h2T, t_h2T = h2T2[blk % 2], t_h2T2[blk % 2]
                for f in range(NF):
                    pr = f % 2
                    bg, bu = 2 * pr, 2 * pr + 1
                    for kc in range(8):
                        P.op(pe, lambda kc=kc: nc.tensor.matmul(banks[bg][:, :], wg[:, kc, f * 128:(f + 1) * 128], h2T[:, kc, :],
                                                                start=(kc == 0), stop=(kc == 7)),
                             reads=[t_wg[f // 2]] + t_h2T, writes=[bank_t[bg]], signal=(kc == 7))
                    for kc in range(8):
                        P.op(pe, lambda kc=kc: nc.tensor.matmul(banks[bu][:, :], wu[:, kc, f * 128:(f + 1) * 128], h2T[:, kc, :],
                                                                start=(kc == 0), stop=(kc == 7)),
                             reads=[t_wu[f // 2]] + t_h2T, writes=[bank_t[bu]], signal=(kc == 7))
                    P.op(act, lambda: nc.scalar.activation(out=sg[pr], in_=banks[bg][:, :], func=AF.Silu),
                         reads=[bank_t[bg]], writes=[t_sg[pr]])
                    P.op(dve, lambda: nc.vector.tensor_tensor(out=gT[:, f, :], in0=banks[bu][:, :], in1=sg[pr], op=ALU.mult),
                         reads=[bank_t[bu], t_sg[pr]], writes=[t_gT[f]])
                    if pending:
                        pending.pop(0)()
                while pending:
                    pending.pop(0)()

            def DN_tile(blk, tt):
                r0 = blk * 512 + tt * 128
                xr, t_xr = next_xb()
                P.dma("sp", xr[:], y_d[r0:r0 + 128, :], writes=[t_xr])
                for half in range(2):
                    bk = 4 + 2 * (tt % 2) + half
                    for f in range(NF):
                        P.op(pe, lambda f=f: nc.tensor.matmul(banks[bk][:, :], gT[:, f, tt * 128:(tt + 1) * 128],
                                                              wd[:, f, half * 512:(half + 1) * 512], start=(f == 0), stop=(f == NF - 1)),
                             reads=[t_wd[f // 2], t_gT[f]], writes=[bank_t[bk]], signal=(f == NF - 1))
                    P.op(dve, lambda half=half: nc.vector.tensor_tensor(out=ytmpB[:, half * 512:(half + 1) * 512], in0=banks[bk][:, :],
                                                                        in1=g2cur[:, half * 512:(half + 1) * 512], op=ALU.mult),
                         reads=[bank_t[bk], t_g2cur], writes=[t_ytmpB])
                ops = [
                    lambda: P.op(dve, lambda: nc.vector.tensor_tensor(out=xr[:], in0=xr[:], in1=agbc[:], op=ALU.mult),
                                 reads=[t_xr, t_pB], writes=[t_xr]),
                    lambda: P.op(dve, lambda: nc.vector.tensor_tensor(out=ytmpB[:], in0=ytmpB[:], in1=abbc[:], op=ALU.add),
                                 reads=[t_ytmpB, t_pB], writes=[t_ytmpB]),
                    lambda: P.op(dve, lambda: nc.vector.tensor_tensor(out=xr[:], in0=xr[:], in1=ytmpB[:], op=ALU.add),
                                 reads=[t_xr, t_ytmpB], writes=[t_xr]),
                ]
                ops += layernorm_ops(xr, t_xr, ln2g[:], t_pB, smallB, t_smallB)
                ops.append(lambda: P.op(dve, lambda: nc.vector.tensor_tensor(out=xr[:], in0=xr[:], in1=ln2b[:], op=ALU.add),
                                        reads=[t_xr, t_pB], writes=[t_xr]))
                ops.append(lambda: P.dma("pool", y_d[r0:r0 + 128, :], xr[:], reads=[t_xr]))
                return ops

            NBLK = NBC * 4
            for tt in range(4):
                TB_tile(0, tt)
            pending = []
            for blk in range(NBLK):
                if blk % 4 == 0:
                    bq = blk // 4
                    P.dma("sp", g2cur[:], g2s_d[bq:bq + 1, :].partition_broadcast(128), reads=[t_g2s], writes=[t_g2cur])
                GU(blk, pending)
                for tt in range(4):
                    ops = DN_tile(blk, tt)
                    if blk + 1 < NBLK:
                        TB_tile(blk + 1, tt)
                    if tt < 3:
                        for o in ops:
                            o()
                    else:
                        pending = ops
            while pending:
                pending.pop(0)()
            P.barrier()
    return nc


def _rope_tables():
    half = 8
    inv_freq = (np.float32(ROPE_THETA) ** (-np.arange(half, dtype=np.float32) * np.float32(2.0) / np.float32(16))).astype(np.float32)
    pos = np.arange(S, dtype=np.float32)
    ang = (pos[:, None] * inv_freq[None, :]).astype(np.float32)
    cos = np.cos(ang).astype(np.float32)
    sin = np.sin(ang).astype(np.float32)

    def lay(t):
        t = t.reshape(16, 128, 1, 8)
        t = np.broadcast_to(t, (16, 128, 8, 8))
        return np.ascontiguousarray(t.transpose(1, 0, 2, 3).reshape(128, 1024))
    return lay(cos), lay(sin)


_NC_CACHE = {}


def kernel(x, c, ada_w, ada_b, w_in, lambda_q1, lambda_k1, lambda_q2, lambda_k2,
           subln_g, sgu_ln_g, sgu_ln_b, sgu_w, sgu_b, w_o, ln1_g, ln1_b,
           w_gate, w_up, w_down, ln2_g, ln2_b):
    f = lambda a: np.ascontiguousarray(np.asarray(a, dtype=np.float32))
    x = f(x)
    c = f(c)
    cos_t, sin_t = _rope_tables()
    ident = np.eye(128, dtype=np.float32)
    tri = np.triu(np.ones((128, 128), dtype=np.float32))
    sel = np.zeros((2, 256), dtype=np.float32)
    sel[0, 0:128] = 1.0
    sel[1, 128:256] = 1.0
    shared = {
        "ada_w": f(ada_w[0]), "ada_b": f(ada_b[0]).reshape(1, -1), "w_in": f(w_in[0]),
        "lam4": f(np.stack([np.asarray(lambda_q1[0]), np.asarray(lambda_k1[0]), np.asarray(lambda_q2[0]), np.asarray(lambda_k2[0])])).reshape(1, 256),
        "subln_g": f(subln_g[0]).reshape(128, 1),
        "sgu_ln_g": f(sgu_ln_g[0]).reshape(1, 512), "sgu_ln_b": f(sgu_ln_b[0]).reshape(1, 512),
        "sgu_wT": f(np.transpose(np.asarray(sgu_w[0]), (0, 2, 1))), "sgu_b": f(sgu_b[0]).reshape(1, 512),
        "w_o": f(w_o[0]), "ln1_g": f(ln1_g[0]).reshape(1, -1), "ln1_b": f(ln1_b[0]).reshape(1, -1),
        "ln1_gT": f(np.asarray(ln1_g[0]).reshape(8, 128).T), "ln1_bT": f(np.asarray(ln1_b[0]).reshape(8, 128).T),
        "w_gate": f(w_gate[0]), "w_up": f(w_up[0]), "w_down": f(w_down[0]),
        "ln2_g": f(ln2_g[0]).reshape(1, -1), "ln2_b": f(ln2_b[0]).reshape(1, -1),
        "ident": ident, "tri": tri, "cos_t": cos_t, "sin_t": sin_t, "sel": sel,
    }
    in_maps = []
    for i in range(NCORES):
        m = dict(shared)
        m["x"] = x[NBC * i:NBC * (i + 1)].reshape(NBC * S, D)
        m["cT"] = np.ascontiguousarray(c[NBC * i:NBC * (i + 1)].T)
        in_maps.append(m)
    if "nc" not in _NC_CACHE:
        _NC_CACHE["nc"] = build_nc()
    nc = _NC_CACHE["nc"]
    res = run_bass_kernel_spmd(nc, in_maps, core_ids=list(range(NCORES)))
    out = np.concatenate([r["y"].reshape(NBC, S, D) for r in res.results], axis=0)
    return out.astype(np.float32)
```

```python
import math
import os
KSTOP = int(os.environ.get('KSTOP', '0'))


class Stop(Exception):
    pass


def chk(n):
    if KSTOP == n:
        raise Stop()
from contextlib import ExitStack
import numpy as np
import concourse.bass as bass
import concourse.mybir as mybir
from concourse.bass_utils import run_bass_kernel_spmd

F32 = mybir.dt.float32
BF16 = mybir.dt.bfloat16
AF = mybir.ActivationFunctionType
ALU = mybir.AluOpType
AX = mybir.AxisListType

D = 1024
S = 2048
NBC = 2
NCORES = 8
DFF = 2816
NF = DFF // 128
PROJ = 2560
ALPHA = 2.0 ** 0.25
EPS = 1e-5
LAM_INIT = 0.2
ROPE_THETA = 500000.0


class Trk:
    __slots__ = ("w", "r", "x")

    def __init__(self, excl=False):
        self.w = None
        self.r = {}
        self.x = excl


class Eng:
    def __init__(self, name, h, sem):
        self.name, self.h, self.sem, self.cnt, self.waited = name, h, sem, 0, {}


class Prog:
    def __init__(self, nc, es):
        self.nc = nc
        mk = lambda n: es.enter_context(nc.semaphore(n))
        self.pe = Eng("pe", nc.tensor, mk("s_pe"))
        self.act = Eng("act", nc.scalar, mk("s_act"))
        self.dve = Eng("dve", nc.vector, mk("s_dve"))
        self.pool = Eng("pool", nc.gpsimd, mk("s_pool"))
        self.sp = Eng("sp", nc.sync, None)
        self.engs = [self.pe, self.act, self.dve, self.pool, self.sp]
        self.sp_sems = [[mk("s_sp%d" % i), 0] for i in range(16)]
        self.pq_sems = [[mk("s_pq%d" % i), 0] for i in range(8)]
        self.sp_i = 0
        self.pq_i = 0
        self.semcnt = {}

    def _emit_waits(self, eng, deps, fn):
        need = []
        for k, (s, v) in deps.items():
            if eng is self.pe and s is self.pe.sem:
                continue
            if eng.waited.get(k, 0) >= v:
                continue
            assert self.semcnt.get(k, (None, 0))[1] >= v, "wait on a signal never emitted (%s)" % eng.name
            need.append((s, v))
            eng.waited[k] = v
        for (s, v) in need[:-1]:
            eng.h.wait_ge(s, v)
        inst = fn()
        if need:
            inst._wait_ge(need[-1][0], need[-1][1])
        return inst

    @staticmethod
    def _deps(reads, writes, extra=(), own=None):
        deps = {}

        def add(ev):
            if ev is None:
                return
            k = id(ev[0])
            if k not in deps or deps[k][1] < ev[1]:
                deps[k] = ev
        for t in reads:
            add(t.w)
            if t.x:
                for ev in t.r.values():
                    if ev[0] is not own:
                        add(ev)
        for t in writes:
            add(t.w)
            for ev in t.r.values():
                add(ev)
        for ev in extra:
            add(ev)
        return deps

    @staticmethod
    def _record(ev, reads, writes):
        for t in writes:
            t.w = ev
            t.r = {}
        for t in reads:
            k = id(ev[0])
            if k not in t.r or t.r[k][1] < ev[1]:
                t.r[k] = ev

    def op(self, eng, fn, reads=(), writes=(), signal=True):
        deps = self._deps(reads, writes, own=eng.sem)
        inst = self._emit_waits(eng, deps, fn)
        if signal:
            eng.cnt += 1
            inst.then_inc(eng.sem, 1)
            self.semcnt[id(eng.sem)] = (eng.sem, eng.cnt)
            ev = (eng.sem, eng.cnt)
        else:
            ev = (eng.sem, eng.cnt + 1)
        self._record(ev, reads, writes)
        return inst

    def dma(self, q, out, in_, reads=(), writes=(), **kw):
        if q == "sp":
            eng, pool = self.sp, self.sp_sems
            i = self.sp_i
            self.sp_i = (i + 1) % len(pool)
        else:
            eng, pool = self.pool, self.pq_sems
            i = self.pq_i
            self.pq_i = (i + 1) % len(pool)
        sem, cnt = pool[i]
        extra = [(sem, cnt)] if cnt > 0 else []
        deps = self._deps(reads, writes, extra)
        inst = self._emit_waits(eng, deps, lambda: eng.h.dma_start(out=out, in_=in_, **kw))
        inst.then_inc(sem, 16)
        pool[i][1] = cnt + 16
        self.semcnt[id(sem)] = (sem, cnt + 16)
        ev = (sem, cnt + 16)
        self._record(ev, reads, writes)

    def all_events(self):
        evs = []
        for e in (self.pe, self.act, self.dve, self.pool):
            if e.cnt > 0:
                evs.append((e.sem, e.cnt))
        for s, c in self.sp_sems + self.pq_sems:
            if c > 0:
                evs.append((s, c))
        return evs

    def barrier(self):
        evs = self.all_events()
        for e in self.engs:
            for (s, v) in evs:
                if e.waited.get(id(s), 0) >= v:
                    continue
                e.h.wait_ge(s, v)
                e.waited[id(s)] = v


def build_nc():
    nc = bass.Bass("TRN2", target_bir_lowering=False)
    dt = lambda n, shp, kind="ExternalInput": nc.dram_tensor(n, shp, F32, kind=kind).ap()
    x_d = dt("x", [NBC * S, D])
    cT_d = dt("cT", [D, NBC])
    adaw_d = dt("ada_w", [D, 6 * D])
    adab_d = dt("ada_b", [1, 6 * D])
    win_d = dt("w_in", [D, PROJ])
    lam_d = dt("lam4", [1, 256])
    subg_d = dt("subln_g", [128, 1])
    sglg_d = dt("sgu_ln_g", [1, 512])
    sglb_d = dt("sgu_ln_b", [1, 512])
    sgwT_d = dt("sgu_wT", [4, 128, 128])
    sgb_d = dt("sgu_b", [1, 512])
    wo_d = dt("w_o", [D, D])
    ln1g_d = dt("ln1_g", [1, D])
    ln1b_d = dt("ln1_b", [1, D])
    ln1gT_d = dt("ln1_gT", [128, 8])
    ln1bT_d = dt("ln1_bT", [128, 8])
    wg_d = dt("w_gate", [D, DFF])
    wu_d = dt("w_up", [D, DFF])
    wd_d = dt("w_down", [DFF, D])
    ln2g_d = dt("ln2_g", [1, D])
    ln2b_d = dt("ln2_b", [1, D])
    ident_d = dt("ident", [128, 128])
    tri_d = dt("tri", [128, 128])
    cos_d = dt("cos_t", [128, 1024])
    sin_d = dt("sin_t", [128, 1024])
    sel_d = dt("sel", [2, 256])
    y_d = dt("y", [NBC * S, D], kind="ExternalOutput")
    g2s_d = dt("g2s", [2, D], kind="ExternalOutput")

    with ExitStack() as es:
        P = Prog(nc, es)
        pe, act, dve, pool = P.pe, P.act, P.dve, P.pool

        def sb(name, shape, dtype, stack=es):
            return stack.enter_context(nc.sbuf_tensor(name, shape, dtype))

        psum_all = es.enter_context(nc.psum_tensor("psum_all", [128, 4096], F32))
        banks = [psum_all[:, i * 512:(i + 1) * 512] for i in range(8)]
        bank_t = [Trk(True) for _ in range(8)]

        ident_f = sb("ident_f", [128, 128], F32)
        epsc = sb("epsc", [128, 1], F32)
        mhalf = sb("mhalf", [128, 1], F32)
        modT = sb("modT", [128, 4, 8, 2], F32)
        t_const = Trk()
        t_modT = Trk()
        t_g2s = Trk()

        P.dma("sp", ident_f[:], ident_d[:, :], writes=[t_const])
        P.op(dve, lambda: nc.vector.memset(epsc[:], EPS), writes=[t_const])
        P.op(dve, lambda: nc.vector.memset(mhalf[:], -0.5), writes=[t_const])

        def evac_copy(i, out, in_, reads, writes):
            if i % 2 == 0:
                P.op(act, lambda: nc.scalar.activation(out=out, in_=in_, func=AF.Copy), reads=reads, writes=writes)
            else:
                P.op(dve, lambda: nc.vector.tensor_copy(out=out, in_=in_), reads=reads, writes=writes)

        def layernorm_inplace(buf, t_buf, width, gbc, bbc, t_gb, small, t_small):
            nchunk = width // 512
            for c in range(nchunk):
                P.op(dve, lambda c=c: nc.vector.bn_stats(out=small[:, c * 6:(c + 1) * 6], in_=buf[:, c * 512:(c + 1) * 512]),
                     reads=[t_buf], writes=[t_small])
            P.op(dve, lambda: nc.vector.bn_aggr(out=small[:, 12:14], in_=small[:, 0:6 * nchunk]), reads=[t_small], writes=[t_small])
            P.op(dve, lambda: nc.vector.tensor_scalar(out=small[:, 14:15], in0=small[:, 13:14], scalar1=EPS, scalar2=None, op0=ALU.add),
                 reads=[t_small], writes=[t_small])
            P.op(pool, lambda: nc.gpsimd.tensor_tensor(out=small[:, 15:16], in0=small[:, 14:15], in1=mhalf[:], op=ALU.pow),
                 reads=[t_small, t_const], writes=[t_small])
            P.op(dve, lambda: nc.vector.tensor_scalar(out=buf[:, 0:width], in0=buf[:, 0:width], scalar1=small[:, 12:13], scalar2=small[:, 15:16],
                                                      op0=ALU.subtract, op1=ALU.mult), reads=[t_buf, t_small], writes=[t_buf])
            if gbc is not None:
                P.op(dve, lambda: nc.vector.tensor_tensor(out=buf[:, 0:width], in0=buf[:, 0:width], in1=gbc, op=ALU.mult),
                     reads=[t_buf, t_gb], writes=[t_buf])

        stopped = [False]
        with ExitStack() as esA:
            try:
                sA = lambda n, shp, d: sb(n, shp, d, esA)
                ident_b = sA("ident_b", [128, 128], BF16)
                tri_f = sA("tri_f", [128, 128], F32)
                tri_b = sA("tri_b", [128, 128], BF16)
                ones_b = sA("ones_b", [128, 128], BF16)
                ones_f = sA("ones_f", [128, 128], F32)
                P.dma("sp", tri_f[:], tri_d[:, :], writes=[t_const])
                P.op(dve, lambda: nc.vector.tensor_copy(out=ident_b[:], in_=ident_f[:]), reads=[t_const], writes=[t_const])
                P.op(dve, lambda: nc.vector.tensor_copy(out=tri_b[:], in_=tri_f[:]), reads=[t_const], writes=[t_const])
                P.op(dve, lambda: nc.vector.memset(ones_b[:], 1.0), writes=[t_const])
                P.op(dve, lambda: nc.vector.memset(ones_f[:], 1.0), writes=[t_const])
                w_in = sA("w_in_sb", [128, 8, PROJ], BF16)
                w_o = sA("w_o_sb", [128, 8, D], BF16)
                t_win = [Trk() for _ in range(8)]
                t_wo = [Trk() for _ in range(8)]
                g1bc = sA("g1bc", [128, 2, 1024], F32)
                t_g1 = Trk()
                sglg = sA("sglg", [128, 512], F32)
                sglb = sA("sglb", [128, 512], F32)
                cos_t = sA("cos_sb", [128, 1024], F32)
                sin_t = sA("sin_sb", [128, 1024], F32)
                WT = sA("WT", [128, 4, 128], BF16)
                sgub = sA("sgub", [1, 512], F32)
                Pt2 = [sA("Pt2_%d" % i, [128, 2, 512], BF16) for i in range(2)]
                t_Pt2 = [Trk(), Trk()]
                Pt = [[Pt2[i][:, c, :] for i in range(2)] for c in range(2)]
                t_Pt = [[t_Pt2[i] for i in range(2)] for c in range(2)]
                tri_b2 = sA("tri_b2", [128, 2, 128], BF16)
                for c_ in range(2):
                    P.op(dve, lambda c_=c_: nc.vector.tensor_copy(out=tri_b2[:, c_, :], in_=tri_f[:]), reads=[t_const], writes=[t_const])
                lam4 = Pt2[0][:, 0, :].bitcast(F32)
                lamt = Pt2[1][:, 0, :].bitcast(F32)[:, 0:128]
                sel = g1bc[0:2, 1, 768:1024]
                lams = sA("lams", [128, 4], F32)
                nlam = sA("nlam", [128, 1], F32)
                gs = sA("gs", [128, 1], F32)
                t_pA = Trk()
                cTs = sA("cTs", [128, 8, 2], F32)
                cact = sA("cact", [128, 8, 2], BF16)
                t_c = Trk()
                t_stage = [Trk() for _ in range(2)]
                fin = [[sA("fin%d_%d" % (j, i), [128, 512], F32) for i in range(4)] for j in range(2)]
                t_fin = [[Trk() for _ in range(4)] for _ in range(2)]
                adab = [fin[1][i][0:1, :] for i in range(2)]
                t_adab = [t_fin[1][i] for i in range(2)]
                modblk = [fin[1][2 + i][0:2, :] for i in range(2)]
                t_modblk = [t_fin[1][2 + i] for i in range(2)]

                NXB = 5
                xbuf = [sA("xbuf%d" % i, [128, D], F32) for i in range(NXB)]
                t_xbuf = [Trk() for _ in range(NXB)]
                xb_i = [0]
                ytmp2 = [sA("ytmp%d" % i, [128, D], F32) for i in range(2)]
                t_ytmp2 = [Trk(), Trk()]
                small = sA("small", [128, 16], F32)
                t_small = Trk()
                small2 = sA("small2", [128, 16], F32)
                t_small2 = Trk()
                hT = sA("hT", [128, 8, 512], BF16)
                t_hT = [Trk() for _ in range(4)]
                uT = sA("uT", [128, 4, 512], BF16)
                t_uT = Trk()
                catT = sA("catT", [128, 8, 512], BF16)
                t_cat_att = [Trk() for _ in range(4)]
                t_cat_sgu = [Trk() for _ in range(4)]
                qT = sA("qT", [128, 4, 512], BF16)
                t_qT = [Trk() for _ in range(4)]
                kT = sA("kT", [128, 4, S], BF16)
                t_kT = [Trk() for _ in range(16)]
                stage = [kT[:, 2 * i:2 * i + 2, :].rearrange("p h (a n) -> p (h a) n", n=512) for i in range(2)]
                V = sA("V", [128, 16, 512], BF16)
                t_V = [Trk() for _ in range(16)]
                qk_tm2 = [sA("qk_tm%d" % i, [128, 1024], BF16) for i in range(2)]
                t_qktm2 = [[Trk(), Trk()] for _ in range(2)]
                rtmp = sA("rtmp", [128, 8, 64], F32)
                t_rtmp = [Trk(), Trk()]
                gsv = sA("gsv", [128, 512], F32)
                t_gsv = Trk()
                vln2 = [sA("vln%d" % i, [128, 512], BF16) for i in range(2)]
                t_vln2 = [Trk(), Trk()]

                chk(1)
                P.dma("sp", cTs[:], cT_d.rearrange("(kc p) b -> p kc b", p=128), writes=[t_c])
                P.dma("sp", sel, sel_d[:, :], writes=[t_g1])
                P.op(act, lambda: nc.scalar.activation(out=cact[:], in_=cTs[:], func=AF.Silu), reads=[t_c], writes=[t_c])

                chk(2)
                win_loaded = [False]

                def load_w_in():
                    for kc in range(8):
                        P.dma("pool", w_in[:, kc, :], win_d[kc * 128:(kc + 1) * 128, :], writes=[t_win[kc]], max_dma_last_dim=4096)

                def mod_block(j, sidx, stage_kc, t_stg, load_stage, bk, tbk, bbk):
                    load_stage(j)
                    P.dma("sp", adab[sidx], adab_d[0:1, j * 512:(j + 1) * 512], writes=[t_adab[sidx]])
                    for kc in range(8):
                        P.op(pe, lambda kc=kc: nc.tensor.matmul(banks[bk][0:2, :], cact[:, kc, :], stage_kc(kc),
                                                                start=(kc == 0), stop=False),
                             reads=[t_c] + t_stg, writes=[bank_t[bk]], signal=False)
                    P.op(pe, lambda: nc.tensor.matmul(banks[bk][0:2, :], ones_f[0:1, 0:2], adab[sidx], start=False, stop=True),
                         reads=[t_const, t_adab[sidx]], writes=[bank_t[bk]])
                    P.op(dve, lambda: nc.vector.tensor_copy(out=modblk[sidx], in_=banks[bk][0:2, :]),
                         reads=[bank_t[bk]], writes=[t_modblk[sidx]])
                    gi, hh = j // 2, j % 2
                    if gi in (0, 1, 3, 4):
                        grp = {0: 0, 1: 1, 3: 2, 4: 3}[gi]
                        for cc in range(4):
                            P.op(pe, lambda cc=cc: nc.tensor.matmul(banks[tbk][:, cc * 2:cc * 2 + 2], modblk[sidx][:, cc * 128:(cc + 1) * 128],
                                                                    ident_f[0:2, 0:2], start=True, stop=True),
                                 reads=[t_modblk[sidx], t_const], writes=[bank_t[tbk]], signal=(cc == 3))
                        src = banks[tbk][:, 0:8].rearrange("p (c b) -> p c b", b=2)
                        dst = modT[:, grp, hh * 4:(hh + 1) * 4, :]
                        if gi in (1, 4):
                            P.op(dve, lambda: nc.vector.tensor_scalar(out=dst, in0=src, scalar1=1.0, scalar2=None, op0=ALU.add),
                                 reads=[bank_t[tbk]], writes=[t_modT])
                        else:
                            P.op(dve, lambda: nc.vector.tensor_copy(out=dst, in_=src), reads=[bank_t[tbk]], writes=[t_modT])
                    elif gi == 2:
                        for b in range(2):
                            xbk = (tbk, bbk)[b]
                            P.op(pe, lambda b=b: nc.tensor.matmul(banks[xbk][:, :], sel[0:2, b * 128:(b + 1) * 128], modblk[sidx],
                                                                  start=True, stop=True),
                                 reads=[t_pA, t_modblk[sidx], t_g1], writes=[bank_t[xbk]])
                            P.op(act, lambda b=b: nc.scalar.activation(out=g1bc[:, b, hh * 512:(hh + 1) * 512], in_=banks[xbk][:, :], func=AF.Copy),
                                 reads=[bank_t[xbk]], writes=[t_g1])
                    else:
                        P.dma("sp", g2s_d[0:2, hh * 512:(hh + 1) * 512], modblk[sidx], reads=[t_modblk[sidx]], writes=[t_g2s])

                for jj, j in enumerate([2, 3, 0, 1]):
                    sidx = jj % 2

                    def ld(j, sidx=sidx):
                        P.dma("pool", stage[sidx], adaw_d[:, j * 512:(j + 1) * 512].rearrange("(kc p) n -> p kc n", p=128),
                              writes=[t_stage[sidx]])
                    mod_block(j, sidx, lambda kc, sidx=sidx: stage[sidx][:, kc, :], [t_stage[sidx]], ld, jj % 2, 2 + (jj % 2), 4 + (jj % 2))
                    if jj == 1:
                        load_w_in()

                chk(4)
                for t in t_kT:
                    for ts in t_stage:
                        for k, ev in ts.r.items():
                            if k not in t.r or t.r[k][1] < ev[1]:
                                t.r[k] = ev
                        if ts.w is not None and (t.w is None or True):
                            t.r[id(ts.w[0])] = ts.w

                chk(5)
                P.dma("sp", cos_t[:], cos_d[:, :], writes=[t_pA])
                P.dma("sp", sin_t[:], sin_d[:, :], writes=[t_pA])
                P.dma("sp", sglg[:], sglg_d[0:1, :].partition_broadcast(128), writes=[t_pA])
                P.dma("sp", sglb[:], sglb_d[0:1, :].partition_broadcast(128), writes=[t_pA])
                P.dma("sp", sgub[:], sgb_d[0:1, :], writes=[t_pA])
                WTf = gsv[:].rearrange("p (g t) -> p g t", t=128)
                P.dma("sp", WTf, sgwT_d.rearrange("g s t -> s g t"), writes=[t_gsv])
                P.dma("sp", lam4, lam_d[0:1, :].partition_broadcast(128), writes=[t_pA, t_Pt[0][0]])
                P.dma("sp", gs[:], subg_d[:, :], writes=[t_pA])
                for g in range(4):
                    P.op(dve, lambda g=g: nc.vector.tensor_tensor(out=WT[:, g, :], in0=WTf[:, g, :], in1=tri_f[:], op=ALU.mult),
                         reads=[t_pA, t_const, t_gsv], writes=[t_pA])
                P.op(dve, lambda: nc.vector.tensor_scalar(out=gs[:], in0=gs[:], scalar1=1.0 - LAM_INIT, scalar2=None, op0=ALU.mult),
                     reads=[t_pA], writes=[t_pA])
                P.op(dve, lambda: nc.vector.tensor_tensor(out=lamt[:, 0:64], in0=lam4[:, 0:64], in1=lam4[:, 64:128], op=ALU.mult),
                     reads=[t_pA, t_Pt[0][0]], writes=[t_pA, t_Pt[0][1]])
                P.op(dve, lambda: nc.vector.tensor_tensor(out=lamt[:, 64:128], in0=lam4[:, 128:192], in1=lam4[:, 192:256], op=ALU.mult),
                     reads=[t_pA, t_Pt[0][0]], writes=[t_pA, t_Pt[0][1]])
                P.op(dve, lambda: nc.vector.tensor_reduce(out=lams[:, 0:2], in_=lamt.rearrange("p (a d) -> p a d", d=64), axis=AX.X, op=ALU.add), reads=[t_pA, t_Pt[0][1]], writes=[t_pA])
                P.op(act, lambda: nc.scalar.activation(out=lams[:, 2:4], in_=lams[:, 0:2], func=AF.Exp), reads=[t_pA], writes=[t_pA])
                P.op(dve, lambda: nc.vector.tensor_tensor(out=nlam[:], in0=lams[:, 3:4], in1=lams[:, 2:3], op=ALU.subtract),
                     reads=[t_pA], writes=[t_pA])
                P.op(dve, lambda: nc.vector.tensor_scalar(out=nlam[:], in0=nlam[:], scalar1=-LAM_INIT, scalar2=None, op0=ALU.add),
                     reads=[t_pA], writes=[t_pA])

                chk(6)
                mm_i = [0]
                tr_i = [0]
                a1_i = [0]

                def next_mm():
                    mm_i[0] = (mm_i[0] + 1) % 4
                    return mm_i[0]

                def next_tr():
                    tr_i[0] ^= 1
                    return 6 + tr_i[0]

                def rope_evac(bk, which, pos_tile, par):
                    qk_tm, t_qktm = qk_tm2[par], t_qktm2[par]
                    src = banks[bk][:, :].rearrange("p (c d) -> p c d", d=64)
                    dst = qk_tm[:, which * 512:(which + 1) * 512].rearrange("p (c d) -> p c d", d=64)
                    cs = cos_t[:, pos_tile * 64:(pos_tile + 1) * 64].rearrange("p (c d) -> p c d", d=8)
                    sn = sin_t[:, pos_tile * 64:(pos_tile + 1) * 64].rearrange("p (c d) -> p c d", d=8)
                    t1, t2 = src[:, :, 0:8], src[:, :, 8:16]
                    r = [rtmp[:, which * 4 + i, :].rearrange("p (c d) -> p c d", d=8) for i in range(4)]
                    tq = t_qktm[which]
                    trt = t_rtmp[which]
                    P.op(act, lambda: nc.scalar.activation(out=dst[:, :, 16:64], in_=src[:, :, 16:64], func=AF.Copy),
                         reads=[bank_t[bk]], writes=[tq])
                    P.op(dve, lambda: nc.vector.tensor_tensor(out=r[0], in0=t1, in1=cs, op=ALU.mult), reads=[bank_t[bk], t_pA], writes=[trt])
                    P.op(dve, lambda: nc.vector.tensor_tensor(out=r[1], in0=t2, in1=sn, op=ALU.mult), reads=[bank_t[bk], t_pA], writes=[trt])
                    P.op(dve, lambda: nc.vector.tensor_tensor(out=r[2], in0=t1, in1=sn, op=ALU.mult), reads=[bank_t[bk], t_pA], writes=[trt])
                    P.op(dve, lambda: nc.vector.tensor_tensor(out=r[3], in0=t2, in1=cs, op=ALU.mult), reads=[bank_t[bk], t_pA], writes=[trt])
                    P.op(dve, lambda: nc.vector.tensor_tensor(out=dst[:, :, 0:8], in0=r[0], in1=r[1], op=ALU.subtract), reads=[trt], writes=[tq])
                    P.op(dve, lambda: nc.vector.tensor_tensor(out=dst[:, :, 8:16], in0=r[2], in1=r[3], op=ALU.add), reads=[trt], writes=[tq])

                def A1_tile(gb, tt):
                    b, tb = divmod(gb, 4)
                    xi = xb_i[0]
                    xb_i[0] = (xi + 1) % NXB
                    r0 = gb * 512 + tt * 128
                    xt, t_xt = xbuf[xi], t_xbuf[xi]
                    P.dma("sp", xt[:], x_d[r0:r0 + 128, :], writes=[t_xt])
                    for half in range(2):
                        a1_i[0] = (a1_i[0] + 1) % 4
                        bk = 4 + a1_i[0]
                        for cc in range(4):
                            kc = half * 4 + cc
                            P.op(pe, lambda cc=cc, kc=kc: nc.tensor.transpose(banks[bk][:, cc * 128:(cc + 1) * 128],
                                                                              xt[:, kc * 128:(kc + 1) * 128], ident_f[:]),
                                 reads=[t_xt, t_const], writes=[bank_t[bk]], signal=(cc == 3))
                        for cc in range(4):
                            kc = half * 4 + cc
                            o = hT[:, kc, tt * 128:(tt + 1) * 128]
                            i_ = banks[bk][:, cc * 128:(cc + 1) * 128]
                            sc = modT[:, 1, kc, b:b + 1]
                            sh = modT[:, 0, kc, b:b + 1]
                            if bk % 2 == 0:
                                P.op(act, lambda o=o, i_=i_, sc=sc, sh=sh: nc.scalar.activation(out=o, in_=i_, func=AF.Identity, scale=sc, bias=sh),
                                     reads=[bank_t[bk], t_modT], writes=[t_hT[tt]])
                            else:
                                P.op(dve, lambda o=o, i_=i_, sc=sc, sh=sh: nc.vector.tensor_scalar(out=o, in0=i_, scalar1=sc, scalar2=sh,
                                                                                                    op0=ALU.mult, op1=ALU.add),
                                     reads=[bank_t[bk], t_modT], writes=[t_hT[tt]])

                def A2a(gb):
                    for j in range(4):
                        bk = next_mm()
                        for kc in range(8):
                            P.op(pe, lambda kc=kc: nc.tensor.matmul(banks[bk][:, :], w_in[:, kc, 1536 + j * 128:1536 + (j + 1) * 128],
                                                                    hT[:, kc, :], start=(kc == 0), stop=(kc == 7)),
                                 reads=[t_win[kc]] + t_hT, writes=[bank_t[bk]], signal=(kc == 7))
                        P.op(act, lambda: nc.scalar.activation(out=uT[:, j, :], in_=banks[bk][:, :], func=AF.Gelu),
                             reads=[bank_t[bk]], writes=[t_uT])

                def A2b_mm(gb, tt):
                    b, tb = divmod(gb, 4)
                    ptile = tb * 4 + tt
                    vln, t_vln = vln2[tt % 2], t_vln2[tt % 2]
                    for grp, col0 in (("q", 0), ("k", 512), ("v", 1024), ("sv", 2048)):
                        bk = next_mm()
                        for kc in range(8):
                            P.op(pe, lambda kc=kc: nc.tensor.matmul(banks[bk][:, :], hT[:, kc, tt * 128:(tt + 1) * 128],
                                                                    w_in[:, kc, col0:col0 + 512], start=(kc == 0), stop=(kc == 7)),
                                 reads=[t_win[kc], t_hT[tt]], writes=[bank_t[bk]], signal=(kc == 7))
                        if grp == "q":
                            rope_evac(bk, 0, ptile, tt % 2)
                        elif grp == "k":
                            rope_evac(bk, 1, ptile, tt % 2)
                        elif grp == "v":
                            P.op(act, lambda: nc.scalar.activation(out=V[:, ptile, :], in_=banks[bk][:, :], func=AF.Copy), reads=[bank_t[bk]], writes=[t_V[ptile]])
                        else:
                            P.op(act, lambda: nc.scalar.activation(out=gsv[:], in_=banks[bk][:, :], func=AF.Gelu),
                                 reads=[bank_t[bk]], writes=[t_gsv])
                            P.op(dve, lambda: nc.vector.bn_stats(out=small[:, 0:6], in_=gsv[:]), reads=[t_gsv], writes=[t_small])
                            P.op(dve, lambda: nc.vector.bn_aggr(out=small[:, 12:14], in_=small[:, 0:6]), reads=[t_small], writes=[t_small])
                            P.op(dve, lambda: nc.vector.tensor_scalar(out=small[:, 14:15], in0=small[:, 13:14], scalar1=EPS, scalar2=None,
                                                                      op0=ALU.add), reads=[t_small], writes=[t_small])
                            P.op(pool, lambda: nc.gpsimd.tensor_tensor(out=small[:, 15:16], in0=small[:, 14:15], in1=mhalf[:], op=ALU.pow),
                                 reads=[t_small, t_const], writes=[t_small])
                            P.op(dve, lambda: nc.vector.tensor_scalar(out=gsv[:], in0=gsv[:], scalar1=small[:, 12:13], scalar2=small[:, 15:16],
                                                                      op0=ALU.subtract, op1=ALU.mult), reads=[t_gsv, t_small], writes=[t_gsv])
                            P.op(dve, lambda: nc.vector.tensor_tensor(out=gsv[:], in0=gsv[:], in1=sglg[:], op=ALU.mult),
                                 reads=[t_gsv, t_pA], writes=[t_gsv])
                            P.op(dve, lambda: nc.vector.tensor_tensor(out=vln[:], in0=gsv[:], in1=sglb[:], op=ALU.add),
                                 reads=[t_gsv, t_pA], writes=[t_vln])

                def A2b_dep(gb, tt):
                    b, tb = divmod(gb, 4)
                    ptile = tb * 4 + tt
                    vln, t_vln = vln2[tt % 2], t_vln2[tt % 2]
                    qk_tm, t_qktm = qk_tm2[tt % 2], t_qktm2[tt % 2]
                    for which in range(2):
                        tbk = next_tr()
                        bfv = banks[tbk][:, :].bitcast(BF16)
                        for h in range(4):
                            P.op(pe, lambda h=h: nc.tensor.transpose(bfv[:, h * 128:(h + 1) * 128],
                                                                     qk_tm[:, which * 512 + h * 128:which * 512 + (h + 1) * 128], ident_b[:]),
                                 reads=[t_qktm[which], t_const], writes=[bank_t[tbk]], signal=(h == 3))
                        srcv = bfv[:, 0:512].rearrange("p (h t) -> p h t", t=128)
                        if which == 0:
                            P.op(act, lambda: nc.scalar.activation(out=qT[:, :, tt * 128:(tt + 1) * 128], in_=srcv, func=AF.Copy),
                                 reads=[bank_t[tbk]], writes=[t_qT[tt]])
                        else:
                            P.op(act, lambda: nc.scalar.activation(out=kT[:, :, ptile * 128:(ptile + 1) * 128], in_=srcv, func=AF.Copy),
                                 reads=[bank_t[tbk]], writes=[t_kT[ptile]])
                    sbk = next_tr()
                    for g in range(4):
                        P.op(pe, lambda g=g: nc.tensor.matmul(banks[sbk][:, g * 128:(g + 1) * 128], vln[:, g * 128:(g + 1) * 128],
                                                              WT[:, g, :], start=True, stop=False),
                             reads=[t_vln, t_pA], writes=[bank_t[sbk]], signal=False)
                        P.op(pe, lambda g=g: nc.tensor.matmul(banks[sbk][:, g * 128:(g + 1) * 128], ones_f[0:1, :],
                                                              sgub[0:1, g * 128:(g + 1) * 128], start=False, stop=True),
                             reads=[t_const, t_pA], writes=[bank_t[sbk]], signal=(g == 3))
                    P.op(dve, lambda: nc.vector.tensor_tensor(out=catT[:, 4:8, tt * 128:(tt + 1) * 128],
                                                              in0=banks[sbk][:, :].rearrange("p (g t) -> p g t", t=128),
                                                              in1=uT[:, :, tt * 128:(tt + 1) * 128], op=ALU.mult),
                         reads=[bank_t[sbk], t_uT], writes=[t_cat_sgu[tt]])

                def fin_part1(h):
                    fs, tf = fin[h % 2], t_fin[h % 2]
                    P.op(dve, lambda: nc.vector.tensor_copy(out=fs[0][:], in_=banks[4][:, :]), reads=[bank_t[4]], writes=[tf[0]])
                    P.op(act, lambda: nc.scalar.activation(out=fs[2][:], in_=banks[6][:, :], func=AF.Ln), reads=[bank_t[6]], writes=[tf[2]])
                    P.op(dve, lambda: nc.vector.tensor_copy(out=fs[1][:], in_=banks[5][:, :]), reads=[bank_t[5]], writes=[tf[1]])
                    P.op(act, lambda: nc.scalar.activation(out=fs[3][:], in_=banks[7][:, :], func=AF.Ln), reads=[bank_t[7]], writes=[tf[3]])

                def fin_part2_ops(h):
                    fs, tf = fin[h % 2], t_fin[h % 2]
                    ops = []
                    for c in range(2):
                        ops.append(lambda bkf, c=c: P.op(act, lambda: nc.scalar.activation(out=fs[2 + c][:], in_=fs[2 + c][:], func=AF.Exp, scale=-1.0),
                                                        reads=[tf[2 + c]], writes=[tf[2 + c]]))
                        ops.append(lambda bkf, c=c: P.op(dve, lambda: nc.vector.tensor_tensor(out=fs[c][:], in0=fs[c][:], in1=fs[2 + c][:], op=ALU.mult),
                                                        reads=[tf[c], tf[2 + c]], writes=[tf[c]]))
                    ops.append(lambda bkf: P.op(dve, lambda: nc.vector.scalar_tensor_tensor(out=fs[0][:], in0=fs[1][:], scalar=nlam[:, 0:1], in1=fs[0][:],
                                                                                            op0=ALU.mult, op1=ALU.add),
                                                reads=[tf[1], t_pA], writes=[tf[0]]))
                    ops.append(lambda bkf: P.op(dve, lambda: nc.vector.tensor_tensor(out=fs[2][:], in0=fs[0][:], in1=fs[0][:], op=ALU.mult),
                                                reads=[tf[0]], writes=[tf[2]]))

                    def rms(bkf):
                        P.op(pe, lambda: nc.tensor.matmul(banks[bkf][:, :], ones_f[:], fs[2][:], start=True, stop=True),
                             reads=[t_const, tf[2]], writes=[bank_t[bkf]])
                        P.op(act, lambda: nc.scalar.activation(out=fs[3][:], in_=banks[bkf][:, :], func=AF.Ln,
                                                               scale=1.0 / 128.0, bias=epsc[:, 0:1]),
                             reads=[bank_t[bkf], t_const], writes=[tf[3]])
                        P.op(act, lambda: nc.scalar.activation(out=fs[3][:], in_=fs[3][:], func=AF.Exp, scale=-0.5),
                             reads=[tf[3]], writes=[tf[3]])
                    ops.append(rms)
                    ops.append(lambda bkf: P.op(dve, lambda: nc.vector.scalar_tensor_tensor(out=catT[:, h, :], in0=fs[0][:], scalar=gs[:, 0:1], in1=fs[3][:],
                                                                                            op0=ALU.mult, op1=ALU.mult),
                                                reads=[tf[0], tf[3], t_pA], writes=[t_cat_att[h]]))
                    return ops

                a3_pending = [[]]

                def flush_a3():
                    while a3_pending[0]:
                        a3_pending[0].pop(0)(2)

                def A3(gb):
                    b, tb = divmod(gb, 4)
                    nkt = 4 * tb + 4
                    steps = [(h, kt) for h in range(4) for kt in range(nkt)]

                    def q0_of(kt):
                        jd = kt - 4 * tb
                        return 0 if jd <= 0 else jd * 128

                    def S(i):
                        h, kt = steps[i]
                        q0 = q0_of(kt)
                        sb0 = (i % 2) * 2
                        for c in range(2):
                            lo, hi = c * 64, (c + 1) * 64
                            P.op(pe, lambda c=c, lo=lo, hi=hi: nc.tensor.matmul(banks[sb0 + c][:, q0:512], kT[lo:hi, h, kt * 128:(kt + 1) * 128],
                                                                                qT[lo:hi, h, q0:512], start=True, stop=True),
                                 reads=[t_kT[kt]] + t_qT, writes=[bank_t[sb0 + c]])

                    pending = []
                    S(0)
                    for i, (h, kt) in enumerate(steps):
                        if i + 1 < len(steps):
                            S(i + 1)
                        jd = kt - 4 * tb
                        q0 = q0_of(kt)
                        pi = i % 2
                        sb0 = pi * 2
                        s2 = psum_all[:, sb0 * 512:(sb0 + 2) * 512].rearrange("p (c n) -> p c n", n=512)
                        P.op(act, lambda: nc.scalar.activation(out=Pt2[pi][:, :, q0:512], in_=s2[:, :, q0:512], func=AF.Exp, scale=0.125),
                             reads=[bank_t[sb0], bank_t[sb0 + 1]], writes=[t_Pt2[pi]])
                        if jd >= 0:
                            P.op(dve, lambda: nc.vector.tensor_tensor(out=Pt2[pi][:, :, q0:q0 + 128], in0=Pt2[pi][:, :, q0:q0 + 128],
                                                                      in1=tri_b2[:], op=ALU.mult),
                                 reads=[t_Pt2[pi], t_const], writes=[t_Pt2[pi]])
                        for c in range(2):
                            P.op(pe, lambda c=c: nc.tensor.matmul(banks[4 + c][:, q0:512], V[:, kt, h * 128:(h + 1) * 128], Pt[c][pi][:, q0:512],
                                                                  start=(kt == 0), stop=(kt == nkt - 1)),
                                 reads=[t_V[kt], t_Pt[c][pi]], writes=[bank_t[4 + c]], signal=False)
                            P.op(pe, lambda c=c: nc.tensor.matmul(banks[6 + c][:, q0:512], ones_b[:], Pt[c][pi][:, q0:512],
                                                                  start=(kt == 0), stop=(kt == nkt - 1)),
                                 reads=[t_const, t_Pt[c][pi]], writes=[bank_t[6 + c]], signal=True)
                        if kt >= 1:
                            for _ in range(3):
                                if pending:
                                    pending.pop(0)(sb0)
                        if kt == nkt - 1:
                            while pending:
                                pending.pop(0)(sb0)
                            fin_part1(h)
                            pending = fin_part2_ops(h)
                    a3_pending[0] = pending

                def A4_tile(gb, tt):
                    b, tb = divmod(gb, 4)
                    ytmp, t_ytmp = ytmp2[tt % 2], t_ytmp2[tt % 2]
                    xi = xb_i[0]
                    xb_i[0] = (xi + 1) % NXB
                    r0 = gb * 512 + tt * 128
                    xr, t_xr = xbuf[xi], t_xbuf[xi]
                    P.dma("sp", xr[:], x_d[r0:r0 + 128, :], writes=[t_xr])
                    for half in range(2):
                        bk = next_mm()
                        for kc in range(8):
                            rd = [t_wo[kc], t_cat_att[kc]] if kc < 4 else [t_wo[kc], t_cat_sgu[tt]]
                            P.op(pe, lambda kc=kc: nc.tensor.matmul(banks[bk][:, :], catT[:, kc, tt * 128:(tt + 1) * 128],
                                                                    w_o[:, kc, half * 512:(half + 1) * 512], start=(kc == 0), stop=(kc == 7)),
                                 reads=rd, writes=[bank_t[bk]], signal=(kc == 7))
                        P.op(dve, lambda half=half: nc.vector.tensor_tensor(out=ytmp[:, half * 512:(half + 1) * 512], in0=banks[bk][:, :],
                                                                            in1=g1bc[:, b, half * 512:(half + 1) * 512], op=ALU.mult),
                             reads=[bank_t[bk], t_g1], writes=[t_ytmp])
                    P.op(dve, lambda: nc.vector.scalar_tensor_tensor(out=xr[:], in0=xr[:], scalar=ALPHA, in1=ytmp[:], op0=ALU.mult, op1=ALU.add),
                         reads=[t_xr, t_ytmp], writes=[t_xr])
                    layernorm_inplace(xr, t_xr, D, None, None, t_pA, small2, t_small2)
                    P.dma("pool", y_d[r0:r0 + 128, :], xr[:], reads=[t_xr])

                deferred = [4, 5]

                def deferred_load(j):
                    for i in range(4):
                        P.dma("pool", fin[0][i][:].bitcast(BF16).rearrange("p (a n) -> p a n", n=512),
                              adaw_d[i * 256:(i + 1) * 256, j * 512:(j + 1) * 512].rearrange("(a p) n -> p a n", p=128),
                              writes=[t_fin[0][i]])

                def deferred_compute(j):
                    mod_block(j, 0, lambda kc: fin[0][kc // 2][:].bitcast(BF16)[:, (kc % 2) * 512:(kc % 2 + 1) * 512],
                              [t_fin[0][i] for i in range(4)], lambda j: None, 4, 5, 4)

                def emit_deferred():
                    if not deferred:
                        return
                    j = deferred.pop(0)
                    deferred_load(j)
                    deferred_compute(j)

                late = {1: 10, 2: 11, 3: 6, 4: 7, 5: 8, 6: 9}

                NGB = NBC * 4
                for gb in range(NGB):
                    prev = gb - 1
                    if gb in late:
                        deferred_load(late[gb])
                    for tt in range(4):
                        A1_tile(gb, tt)
                    chk(7)
                    flush_a3()
                    if gb > 0:
                        A4_tile(prev, 0)
                    A2a(gb)
                    if gb == 0:
                        for kc in range(8):
                            P.dma("pool", w_o[:, kc, :], wo_d[kc * 128:(kc + 1) * 128, :], writes=[t_wo[kc]])
                    emit_deferred()
                    chk(8)
                    if gb > 0:
                        A4_tile(prev, 1)
                    A2b_mm(gb, 0)
                    emit_deferred()
                    if gb > 0:
                        A4_tile(prev, 2)
                    A2b_mm(gb, 1)
                    A2b_dep(gb, 0)
                    emit_deferred()
                    if gb > 0:
                        A4_tile(prev, 3)
                    A2b_mm(gb, 2)
                    A2b_dep(gb, 1)
                    emit_deferred()
                    A2b_mm(gb, 3)
                    A2b_dep(gb, 2)
                    A2b_dep(gb, 3)
                    if gb in late:
                        deferred_compute(late[gb])
                    chk(9)
                    A3(gb)
                    chk(10)
                flush_a3()
                for tt in range(4):
                    A4_tile(NGB - 1, tt)
            except Stop:
                stopped[0] = True
            P.barrier()

        with ExitStack() as esB:
          if not stopped[0]:
            sB = lambda n, shp, d: sb(n, shp, d, esB)
            wg = sB("wg_sb", [128, 8, DFF], BF16)
            wu = sB("wu_sb", [128, 8, DFF], BF16)
            wd = sB("wd_sb", [128, NF, D], BF16)
            t_wg = [Trk() for _ in range(11)]
            t_wu = [Trk() for _ in range(11)]
            t_wd = [Trk() for _ in range(11)]
            ln2g = sB("ln2g", [128, D], F32)
            ln2b = sB("ln2b", [128, D], F32)
            t_pB = Trk()
            h2T2 = [sB("h2T%d" % i, [128, 8, 512], BF16) for i in range(2)]
            t_h2T2 = [[Trk() for _ in range(4)] for _ in range(2)]
            gT = sB("gT", [128, NF, 512], BF16)
            t_gT = [Trk() for _ in range(NF)]
            NXBB = 2
            xbB = [sB("xbB_%d" % i, [128, D], F32) for i in range(NXBB)]
            t_xbB = [Trk() for _ in range(NXBB)]
            xbB_i = [0]
            ytmpB = sB("ytmpB", [128, D], F32)
            t_ytmpB = Trk()
            sgy = sB("sgy", [128, D], F32)
            sg = [sgy[:, 0:512], sgy[:, 512:1024]]
            t_sg = [Trk(), Trk()]
            g2cur = sB("g2cur", [128, D], F32)
            t_g2cur = Trk()
            agbc = sB("agbc", [128, D], F32)
            abbc = sB("abbc", [128, D], F32)
            gfm = sB("gfm", [128, 8], F32)
            bfm = sB("bfm", [128, 8], F32)
            modB = sB("modB", [128, 2, 8, 2], F32)
            smallB = sB("smallB", [128, 16], F32)
            t_smallB = Trk()

            P.dma("sp", ln2g[:], ln2g_d[0:1, :].partition_broadcast(128), writes=[t_pB])
            P.dma("sp", ln2b[:], ln2b_d[0:1, :].partition_broadcast(128), writes=[t_pB])
            P.dma("sp", agbc[:], ln1g_d[0:1, :].partition_broadcast(128), writes=[t_pB])
            P.dma("sp", abbc[:], ln1b_d[0:1, :].partition_broadcast(128), writes=[t_pB])
            P.dma("sp", gfm[:], ln1gT_d[:, :], writes=[t_pB])
            P.dma("sp", bfm[:], ln1bT_d[:, :], writes=[t_pB])
            P.op(dve, lambda: nc.vector.tensor_scalar(out=agbc[:], in0=agbc[:], scalar1=ALPHA, scalar2=None, op0=ALU.mult), reads=[t_pB], writes=[t_pB])
            P.op(dve, lambda: nc.vector.tensor_scalar(out=abbc[:], in0=abbc[:], scalar1=ALPHA, scalar2=None, op0=ALU.mult), reads=[t_pB], writes=[t_pB])
            for b_ in range(2):
                P.op(dve, lambda b_=b_: nc.vector.tensor_tensor(out=modB[:, 0, :, b_], in0=modT[:, 3, :, b_], in1=gfm[:], op=ALU.mult),
                     reads=[t_modT, t_pB], writes=[t_pB])
                P.op(dve, lambda b_=b_: nc.vector.tensor_tensor(out=modB[:, 1, :, b_], in0=modT[:, 3, :, b_], in1=bfm[:], op=ALU.mult),
                     reads=[t_modT, t_pB], writes=[t_pB])
                P.op(dve, lambda b_=b_: nc.vector.tensor_tensor(out=modB[:, 1, :, b_], in0=modB[:, 1, :, b_], in1=modT[:, 2, :, b_], op=ALU.add),
                     reads=[t_modT, t_pB], writes=[t_pB])
            wg_v = wg_d.rearrange("(kc p) n -> p kc n", p=128)
            wu_v = wu_d.rearrange("(kc p) n -> p kc n", p=128)
            wd_v = wd_d.rearrange("(f p) n -> p f n", p=128)
            for cgi in range(11):
                c0, c1 = cgi * 256, (cgi + 1) * 256
                P.dma("pool", wg[:, :, c0:c1], wg_v[:, :, c0:c1], writes=[t_wg[cgi]])
                P.dma("pool", wu[:, :, c0:c1], wu_v[:, :, c0:c1], writes=[t_wu[cgi]])
            for cgi in range(11):
                P.dma("pool", wd[:, 2 * cgi:2 * cgi + 2, :], wd_v[:, 2 * cgi:2 * cgi + 2, :], writes=[t_wd[cgi]])

            tbB = [0]

            def next_xb():
                i = xbB_i[0]
                xbB_i[0] = (i + 1) % NXBB
                return xbB[i], t_xbB[i]

            def TB_tile(blk, tt):
                b = blk // 4
                h2T, t_h2T = h2T2[blk % 2], t_h2T2[blk % 2]
                r0 = blk * 512 + tt * 128
                if blk == 0 and tt % 3 != 0:
                    xt, t_one = next_xb()
                    t_x = [t_one]
                else:
                    xt, t_x = sgy, t_sg
                P.dma("sp", xt[:], y_d[r0:r0 + 128, :], writes=t_x)
                for half in range(2):
                    tbB[0] = (tbB[0] + 1) % 4
                    bk = tbB[0]
                    for cc in range(4):
                        kc = half * 4 + cc
                        P.op(pe, lambda cc=cc, kc=kc: nc.tensor.transpose(banks[bk][:, cc * 128:(cc + 1) * 128],
                                                                          xt[:, kc * 128:(kc + 1) * 128], ident_f[:]),
                             reads=t_x + [t_const], writes=[bank_t[bk]], signal=(cc == 3))
                    for cc in range(4):
                        kc = half * 4 + cc
                        o = h2T[:, kc, tt * 128:(tt + 1) * 128]
                        i_ = banks[bk][:, cc * 128:(cc + 1) * 128]
                        sc = modB[:, 0, kc, b:b + 1]
                        sh = modB[:, 1, kc, b:b + 1]
                        if blk == 0 and bk % 2 == 1:
                            P.op(dve, lambda o=o, i_=i_, sc=sc, sh=sh: nc.vector.tensor_scalar(out=o, in0=i_, scalar1=sc, scalar2=sh,
                                                                                                op0=ALU.mult, op1=ALU.add),
                                 reads=[bank_t[bk], t_pB], writes=[t_h2T[tt]])
                        else:
                            P.op(act, lambda o=o, i_=i_, sc=sc, sh=sh: nc.scalar.activation(out=o, in_=i_, func=AF.Identity, scale=sc, bias=sh),
                                 reads=[bank_t[bk], t_pB], writes=[t_h2T[tt]])

            def layernorm_ops(buf, t_buf, gbc, t_gb, small, t_small):
                ops = []
                for c in range(2):
                    ops.append(lambda c=c: P.op(dve, lambda: nc.vector.bn_stats(out=small[:, c * 6:(c + 1) * 6], in_=buf[:, c * 512:(c + 1) * 512]),
                                                reads=[t_buf], writes=[t_small]))
                ops.append(lambda: P.op(dve, lambda: nc.vector.bn_aggr(out=small[:, 12:14], in_=small[:, 0:12]), reads=[t_small], writes=[t_small]))
                ops.append(lambda: P.op(dve, lambda: nc.vector.tensor_scalar(out=small[:, 14:15], in0=small[:, 13:14], scalar1=EPS, scalar2=None, op0=ALU.add),
                                        reads=[t_small], writes=[t_small]))
                ops.append(lambda: P.op(pool, lambda: nc.gpsimd.tensor_tensor(out=small[:, 15:16], in0=small[:, 14:15], in1=mhalf[:], op=ALU.pow),
                                        reads=[t_small, t_const], writes=[t_small]))
                ops.append(lambda: P.op(dve, lambda: nc.vector.tensor_scalar(out=buf[:], in0=buf[:], scalar1=small[:, 12:13], scalar2=small[:, 15:16],
                                                                             op0=ALU.subtract, op1=ALU.mult), reads=[t_buf, t_small], writes=[t_buf]))
                ops.append(lambda: P.op(dve, lambda: nc.vector.tensor_tensor(out=buf[:], in0=buf[:], in1=gbc, op=ALU.mult),
                                        reads=[t_buf, t_gb], writes=[t_buf]))
                return ops

            def GU(blk, pending):
                h2T, t_h2T = h2T2[blk % 2], t_h2T2[blk % 2]
                for f in range(NF):
                    pr = f % 2
                    bg, bu = 2 * pr, 2 * pr + 1
                    for kc in range(8):
                        P.op(pe, lambda kc=kc: nc.tensor.matmul(banks[bg][:, :], wg[:, kc, f * 128:(f + 1) * 128], h2T[:, kc, :],
                                                                start=(kc == 0), stop=(kc == 7)),
                             reads=[t_wg[f // 2]] + t_h2T, writes=[bank_t[bg]], signal=(kc == 7))
                    for kc in range(8):
                        P.op(pe, lambda kc=kc: nc.tensor.matmul(banks[bu][:, :], wu[:, kc, f * 128:(f + 1) * 128], h2T[:, kc, :],
                                                                start=(kc == 0), stop=(kc == 7)),
                             reads=[t_wu[f // 2]] + t_h2T, writes=[bank_t[bu]], signal=(kc == 7))
                    P.op(act, lambda: nc.scalar.activation(out=sg[pr], in_=banks[bg][:, :], func=AF.Silu),
                         reads=[bank_t[bg]], writes=[t_sg[pr]])
                    P.op(dve, lambda: nc.vector.tensor_tensor(out=gT[:, f, :], in0=banks[bu][:, :], in1=sg[pr], op=ALU.mult),
                         reads=[bank_t[bu], t_sg[pr]], writes=[t_gT[f]])
                    if pending:
                        pending.pop(0)()
                while pending:
                    pending.pop(0)()

            def DN_tile(blk, tt):
                r0 = blk * 512 + tt * 128
                xr, t_xr = next_xb()
                P.dma("sp", xr[:], y_d[r0:r0 + 128, :], writes=[t_xr])
                for half in range(2):
                    bk = 4 + 2 * (tt % 2) + half
                    for f in range(NF):
                        P.op(pe, lambda f=f: nc.tensor.matmul(banks[bk][:, :], gT[:, f, tt * 128:(tt + 1) * 128],
                                                              wd[:, f, half * 512:(half + 1) * 512], start=(f == 0), stop=(f == NF - 1)),
                             reads=[t_wd[f // 2], t_gT[f]], writes=[bank_t[bk]], signal=(f == NF - 1))
                    P.op(dve, lambda half=half: nc.vector.tensor_tensor(out=ytmpB[:, half * 512:(half + 1) * 512], in0=banks[bk][:, :],
                                                                        in1=g2cur[:, half * 512:(half + 1) * 512], op=ALU.mult),
                         reads=[bank_t[bk], t_g2cur], writes=[t_ytmpB])
                ops = [
                    lambda: P.op(dve, lambda: nc.vector.tensor_tensor(out=xr[:], in0=xr[:], in1=agbc[:], op=ALU.mult),
                                 reads=[t_xr, t_pB], writes=[t_xr]),
                    lambda: P.op(dve, lambda: nc.vector.tensor_tensor(out=ytmpB[:], in0=ytmpB[:], in1=abbc[:], op=ALU.add),
                                 reads=[t_ytmpB, t_pB], writes=[t_ytmpB]),
                    lambda: P.op(dve, lambda: nc.vector.tensor_tensor(out=xr[:], in0=xr[:], in1=ytmpB[:], op=ALU.add),
                                 reads=[t_xr, t_ytmpB], writes=[t_xr]),
                ]
                ops += layernorm_ops(xr, t_xr, ln2g[:], t_pB, smallB, t_smallB)
                ops.append(lambda: P.op(dve, lambda: nc.vector.tensor_tensor(out=xr[:], in0=xr[:], in1=ln2b[:], op=ALU.add),
                                        reads=[t_xr, t_pB], writes=[t_xr]))
                ops.append(lambda: P.dma("pool", y_d[r0:r0 + 128, :], xr[:], reads=[t_xr]))
                return ops

            NBLK = NBC * 4
            for tt in range(4):
                TB_tile(0, tt)
            pending = []
            for blk in range(NBLK):
                if blk % 4 == 0:
                    bq = blk // 4
                    P.dma("sp", g2cur[:], g2s_d[bq:bq + 1, :].partition_broadcast(128), reads=[t_g2s], writes=[t_g2cur])
                GU(blk, pending)
                for tt in range(4):
                    ops = DN_tile(blk, tt)
                    if blk + 1 < NBLK:
                        TB_tile(blk + 1, tt)
                    if tt < 3:
                        for o in ops:
                            o()
                    else:
                        pending = ops
            while pending:
                pending.pop(0)()
            P.barrier()
    return nc


def _rope_tables():
    half = 8
    inv_freq = (np.float32(ROPE_THETA) ** (-np.arange(half, dtype=np.float32) * np.float32(2.0) / np.float32(16))).astype(np.float32)
    pos = np.arange(S, dtype=np.float32)
    ang = (pos[:, None] * inv_freq[None, :]).astype(np.float32)
    cos = np.cos(ang).astype(np.float32)
    sin = np.sin(ang).astype(np.float32)

    def lay(t):
        t = t.reshape(16, 128, 1, 8)
        t = np.broadcast_to(t, (16, 128, 8, 8))
        return np.ascontiguousarray(t.transpose(1, 0, 2, 3).reshape(128, 1024))
    return lay(cos), lay(sin)


_NC_CACHE = {}


def kernel(x, c, ada_w, ada_b, w_in, lambda_q1, lambda_k1, lambda_q2, lambda_k2,
           subln_g, sgu_ln_g, sgu_ln_b, sgu_w, sgu_b, w_o, ln1_g, ln1_b,
           w_gate, w_up, w_down, ln2_g, ln2_b):
    f = lambda a: np.ascontiguousarray(np.asarray(a, dtype=np.float32))
    x = f(x)
    c = f(c)
    cos_t, sin_t = _rope_tables()
    ident = np.eye(128, dtype=np.float32)
    tri = np.triu(np.ones((128, 128), dtype=np.float32))
    sel = np.zeros((2, 256), dtype=np.float32)
    sel[0, 0:128] = 1.0
    sel[1, 128:256] = 1.0
    shared = {
        "ada_w": f(ada_w[0]), "ada_b": f(ada_b[0]).reshape(1, -1), "w_in": f(w_in[0]),
        "lam4": f(np.stack([np.asarray(lambda_q1[0]), np.asarray(lambda_k1[0]), np.asarray(lambda_q2[0]), np.asarray(lambda_k2[0])])).reshape(1, 256),
        "subln_g": f(subln_g[0]).reshape(128, 1),
        "sgu_ln_g": f(sgu_ln_g[0]).reshape(1, 512), "sgu_ln_b": f(sgu_ln_b[0]).reshape(1, 512),
        "sgu_wT": f(np.transpose(np.asarray(sgu_w[0]), (0, 2, 1))), "sgu_b": f(sgu_b[0]).reshape(1, 512),
        "w_o": f(w_o[0]), "ln1_g": f(ln1_g[0]).reshape(1, -1), "ln1_b": f(ln1_b[0]).reshape(1, -1),
        "ln1_gT": f(np.asarray(ln1_g[0]).reshape(8, 128).T), "ln1_bT": f(np.asarray(ln1_b[0]).reshape(8, 128).T),
        "w_gate": f(w_gate[0]), "w_up": f(w_up[0]), "w_down": f(w_down[0]),
        "ln2_g": f(ln2_g[0]).reshape(1, -1), "ln2_b": f(ln2_b[0]).reshape(1, -1),
        "ident": ident, "tri": tri, "cos_t": cos_t, "sin_t": sin_t, "sel": sel,
    }
    in_maps = []
    for i in range(NCORES):
        m = dict(shared)
        m["x"] = x[NBC * i:NBC * (i + 1)].reshape(NBC * S, D)
        m["cT"] = np.ascontiguousarray(c[NBC * i:NBC * (i + 1)].T)
        in_maps.append(m)
    if "nc" not in _NC_CACHE:
        _NC_CACHE["nc"] = build_nc()
    nc = _NC_CACHE["nc"]
    res = run_bass_kernel_spmd(nc, in_maps, core_ids=list(range(NCORES)))
    out = np.concatenate([r["y"].reshape(NBC, S, D) for r in res.results], axis=0)
    return out.astype(np.float32)
```
